# Optimizing a Trainium2 kernel written in Bass

```python
import math
import jax
import jax.numpy as jnp
from jax import lax
import numpy as np

D_MODEL = 2048
BATCH = 4
SEQ = 8192
DEPTH = 4

CONV_WIDTH = 4
CHUNK = 64
N_BRANCH = 3
BRANCH_WIDTH = D_MODEL // 2
LRU_WIDTH = BRANCH_WIDTH
LRU_BLOCKS = 16
LRU_BLOCK_DIM = LRU_WIDTH // LRU_BLOCKS
LRU_C = 8.0
GLA_HEADS = 4
GLA_DK = D_MODEL // 16
GLA_DV = BRANCH_WIDTH // GLA_HEADS
GLA_LOWRANK = 16
GLA_TAU = 16.0
DN_HEADS = 8
DN_DK = D_MODEL // 16
DN_DV = BRANCH_WIDTH // DN_HEADS
D_FF = 4 * D_MODEL
DEEPNORM_ALPHA = (2.0 * DEPTH) ** 0.25
DEEPNORM_BETA = (8.0 * DEPTH) ** -0.25
LN_EPS = 1e-5
NORM_EPS = 1e-6
IN_SPLITS = (
    LRU_WIDTH,
    LRU_WIDTH,
    GLA_HEADS * GLA_DK,
    GLA_HEADS * GLA_DK,
    GLA_HEADS * GLA_DV,
    GLA_LOWRANK,
    GLA_HEADS * GLA_DV,
    DN_HEADS * (2 * DN_DK + DN_DV),
    DN_HEADS,
    DN_HEADS,
    DN_HEADS * DN_DV,
    N_BRANCH * D_MODEL,
)
D_IN = sum(IN_SPLITS)

kernel_name = 'hybrid_rglru_gla_gdn_deepnorm'


def _split_points(sizes):
    pts, acc = [], 0
    for s in sizes[:-1]:
        acc += s
        pts.append(acc)
    return pts


def _layernorm(x, g, b):
    xf = x.astype(jnp.float32)
    mu = jnp.mean(xf, axis=-1, keepdims=True)
    var = jnp.mean(jnp.square(xf - mu), axis=-1, keepdims=True)
    return ((xf - mu) * lax.rsqrt(var + LN_EPS) * g + b).astype(x.dtype)


def _rmsnorm(x, g):
    xf = x.astype(jnp.float32)
    return xf * lax.rsqrt(jnp.mean(xf * xf, axis=-1, keepdims=True) + NORM_EPS) * g.astype(jnp.float32)


def _l2norm(x):
    return x * lax.rsqrt(jnp.sum(x * x, axis=-1, keepdims=True) + NORM_EPS)


def _causal_conv(x, w):
    c = x.shape[-1]
    return lax.conv_general_dilated(
        x, w.astype(x.dtype)[:, None, :], window_strides=(1,),
        padding=[(CONV_WIDTH - 1, 0)], dimension_numbers=('NWC', 'WIO', 'NWC'),
        feature_group_count=c)


def _rg_lru_branch(xb, yb, conv_w, conv_b, wa, ba, wi, bi, lam):
    f32 = jnp.float32
    bsz, seq, _ = xb.shape
    xc = _causal_conv(xb.astype(f32), conv_w) + conv_b.astype(f32)
    xg = xc.reshape(bsz, seq, LRU_BLOCKS, LRU_BLOCK_DIM)
    r = jax.nn.sigmoid(jnp.einsum('bsgi,gij->bsgj', xg, wa.astype(f32)).reshape(bsz, seq, LRU_WIDTH) + ba)
    i = jax.nn.sigmoid(jnp.einsum('bsgi,gij->bsgj', xg, wi.astype(f32)).reshape(bsz, seq, LRU_WIDTH) + bi)
    log_a = -LRU_C * r * jax.nn.softplus(-lam.astype(f32))
    a = jnp.exp(log_a)
    u = jnp.sqrt(-jnp.expm1(2.0 * log_a)) * (i * xc)

    def combine(c1, c2):
        a1, b1 = c1
        a2, b2 = c2
        return a1 * a2, a2 * b1 + b2

    _, h = lax.associative_scan(combine, (a, u), axis=1)
    return h * jax.nn.gelu(yb.astype(f32))


def _gla_branch(q, k, v, alr, r, wa2, ba2, norm_g):
    f32 = jnp.float32
    bsz, seq, _ = q.shape
    n = seq // CHUNK
    shp_k = (bsz, n, CHUNK, GLA_HEADS, GLA_DK)
    shp_v = (bsz, n, CHUNK, GLA_HEADS, GLA_DV)
    gk = jax.nn.log_sigmoid(alr.astype(f32) @ wa2.astype(f32) + ba2) / GLA_TAU
    b = jnp.cumsum(gk.reshape(shp_k), axis=2)
    b_last = b[:, :, -1:]
    qc = q.astype(f32).reshape(shp_k) * (GLA_DK ** -0.5)
    kc = k.astype(f32).reshape(shp_k)
    vc = v.astype(f32).reshape(shp_v)
    qe = qc * jnp.exp(b)
    ke = kc * jnp.exp(-b)
    kd = kc * jnp.exp(b_last - b)
    causal = jnp.tril(jnp.ones((CHUNK, CHUNK), dtype=bool))
    att = jnp.where(causal, jnp.einsum('bnihd,bnjhd->bnhij', qe, ke), 0.0)
    o_intra = jnp.einsum('bnhij,bnjhv->bnihv', att, vc)
    upd = jnp.einsum('bnjhd,bnjhv->bnhdv', kd, vc)
    dec = jnp.exp(b_last[:, :, 0])

    def step(state, inp):
        d, u = inp
        return d[..., None] * state + u, state

    s0 = jnp.zeros((bsz, GLA_HEADS, GLA_DK, GLA_DV), f32)
    _, s_prev = lax.scan(step, s0, (jnp.moveaxis(dec, 1, 0), jnp.moveaxis(upd, 1, 0)))
    s_prev = jnp.moveaxis(s_prev, 0, 1)
    o = o_intra + jnp.einsum('bnihd,bnhdv->bnihv', qe, s_prev)
    o = _rmsnorm(o.reshape(bsz, seq, GLA_HEADS, GLA_DV), norm_g).reshape(bsz, seq, GLA_HEADS * GLA_DV)
    return o * jax.nn.silu(r.astype(f32))


def _gated_deltanet_branch(qkv, beta_logit, a_logit, z, conv_w, a_log, dt_bias, norm_g):
    f32 = jnp.float32
    bsz, seq, _ = qkv.shape
    n = seq // CHUNK
    qkv = jax.nn.silu(_causal_conv(qkv.astype(f32), conv_w))
    q, k, v = jnp.split(qkv, [DN_HEADS * DN_DK, 2 * DN_HEADS * DN_DK], axis=-1)
    q = _l2norm(q.reshape(bsz, seq, DN_HEADS, DN_DK)) * (DN_DK ** -0.5)
    k = _l2norm(k.reshape(bsz, seq, DN_HEADS, DN_DK))
    v = v.reshape(bsz, seq, DN_HEADS, DN_DV)
    beta = jax.nn.sigmoid(beta_logit.astype(f32))
    g = -jnp.exp(a_log.astype(f32)) * jax.nn.softplus(a_logit.astype(f32) + dt_bias)

    def chunks(t):
        return t.reshape(bsz, n, CHUNK, DN_HEADS, -1).transpose(0, 1, 3, 2, 4)

    qc, kc, vc = chunks(q), chunks(k), chunks(v)
    bc = chunks(beta[..., None])[..., 0]
    gc = jnp.cumsum(chunks(g[..., None])[..., 0], axis=-1)
    incl = jnp.tril(jnp.ones((CHUNK, CHUNK), dtype=bool))
    diff = gc[..., :, None] - gc[..., None, :]
    decay = jnp.where(incl, jnp.exp(jnp.where(incl, diff, 0.0)), 0.0)
    kb = kc * bc[..., None]
    lower = jnp.einsum('bnhid,bnhjd->bnhij', kb, kc) * decay
    rhs = jnp.concatenate([vc * bc[..., None], kb * jnp.exp(gc)[..., None]], axis=-1)
    sol = lax.linalg.triangular_solve(lower, rhs, left_side=True, lower=True, unit_diagonal=True)
    value, kcum = sol[..., :DN_DV], sol[..., DN_DV:]
    aqk = jnp.einsum('bnhid,bnhjd->bnhij', qc, kc) * decay
    qg = qc * jnp.exp(gc)[..., None]
    g_last = gc[..., -1]
    kg = kc * jnp.exp(g_last[..., None] - gc)[..., None]

    def step(state, inp):
        val, kcd, qgn, kgn, aq, gl = inp
        v_new = val - jnp.einsum('bhcd,bhdv->bhcv', kcd, state)
        o = jnp.einsum('bhcd,bhdv->bhcv', qgn, state) + jnp.einsum('bhij,bhjv->bhiv', aq, v_new)
        state = state * jnp.exp(gl)[..., None, None] + jnp.einsum('bhcd,bhcv->bhdv', kgn, v_new)
        return state, o

    xs = tuple(jnp.moveaxis(t, 1, 0) for t in (value, kcum, qg, kg, aqk, g_last))
    s0 = jnp.zeros((bsz, DN_HEADS, DN_DK, DN_DV), f32)
    _, o = lax.scan(step, s0, xs)
    o = o.transpose(1, 0, 3, 2, 4).reshape(bsz, seq, DN_HEADS, DN_DV)
    o = _rmsnorm(o, norm_g) * jax.nn.silu(z.astype(f32).reshape(bsz, seq, DN_HEADS, DN_DV))
    return o.reshape(bsz, seq, DN_HEADS * DN_DV)


def _token_mixer(x, w_in, lru_conv_w, lru_conv_b, lru_wa, lru_ba, lru_wi, lru_bi, lru_lambda,
                 gla_wa2, gla_ba2, gla_norm_g, dn_conv_w, dn_a_log, dn_dt_bias, dn_norm_g,
                 w_branch, b_gate, w_out):
    bsz, seq, _ = x.shape
    proj = jnp.einsum('bsd,de->bse', x, w_in)
    (lru_x, lru_y, gla_q, gla_k, gla_v, gla_alr, gla_r,
     dn_qkv, dn_b, dn_a, dn_z, gate_logits) = jnp.split(proj, _split_points(IN_SPLITS), axis=-1)
    y_lru = _rg_lru_branch(lru_x, lru_y, lru_conv_w, lru_conv_b, lru_wa, lru_ba, lru_wi, lru_bi,
                           lru_lambda).astype(x.dtype)
    y_gla = _gla_branch(gla_q, gla_k, gla_v, gla_alr, gla_r, gla_wa2, gla_ba2, gla_norm_g).astype(x.dtype)
    y_dn = _gated_deltanet_branch(dn_qkv, dn_b, dn_a, dn_z, dn_conv_w, dn_a_log, dn_dt_bias,
                                  dn_norm_g).astype(x.dtype)
    gates = jax.nn.sigmoid(gate_logits.reshape(bsz, seq, N_BRANCH, D_MODEL) + b_gate)
    merged = (gates[:, :, 0] * (y_lru @ w_branch[0])
              + gates[:, :, 1] * (y_gla @ w_branch[1])
              + gates[:, :, 2] * (y_dn @ w_branch[2]))
    return merged @ w_out


def _squared_relu_mlp(x, w1, b1, w2, b2):
    return jnp.square(jax.nn.relu(x @ w1 + b1)) @ w2 + b2


def setup_inputs(seed: int = 0) -> dict:
    key = jax.random.key(seed)
    ks = jax.random.split(key, 28)
    f32 = jnp.float32
    L = DEPTH

    def nrm(k, shape, scale):
        return jax.random.normal(k, shape, f32) * scale

    lru_a0 = jax.random.uniform(ks[8], (L, LRU_WIDTH), f32, 0.9, 0.999)
    lru_sig = lru_a0 ** (1.0 / LRU_C)
    dt0 = jnp.exp(jax.random.uniform(ks[13], (L, DN_HEADS), f32, math.log(1e-3), math.log(1e-1)))
    return {
        'x': nrm(ks[0], (BATCH, SEQ, D_MODEL), 1.0),
        'w_in': nrm(ks[1], (L, D_MODEL, D_IN), D_MODEL ** -0.5),
        'lru_conv_w': nrm(ks[2], (L, CONV_WIDTH, LRU_WIDTH), CONV_WIDTH ** -0.5),
        'lru_conv_b': nrm(ks[3], (L, LRU_WIDTH), 0.01),
        'lru_wa': nrm(ks[4], (L, LRU_BLOCKS, LRU_BLOCK_DIM, LRU_BLOCK_DIM), LRU_BLOCK_DIM ** -0.5),
        'lru_ba': nrm(ks[5], (L, LRU_WIDTH), 0.01),
        'lru_wi': nrm(ks[6], (L, LRU_BLOCKS, LRU_BLOCK_DIM, LRU_BLOCK_DIM), LRU_BLOCK_DIM ** -0.5),
        'lru_bi': nrm(ks[7], (L, LRU_WIDTH), 0.01),
        'lru_lambda': jnp.log(lru_sig) - jnp.log1p(-lru_sig),
        'gla_wa2': nrm(ks[9], (L, GLA_LOWRANK, GLA_HEADS * GLA_DK), GLA_LOWRANK ** -0.5),
        'gla_ba2': nrm(ks[10], (L, GLA_HEADS * GLA_DK), 0.01),
        'gla_norm_g': 1.0 + nrm(ks[11], (L, GLA_DV), 0.01),
        'dn_conv_w': nrm(ks[12], (L, CONV_WIDTH, DN_HEADS * (2 * DN_DK + DN_DV)), CONV_WIDTH ** -0.5),
        'dn_a_log': jnp.log(jax.random.uniform(ks[14], (L, DN_HEADS), f32, 1.0, 16.0)),
        'dn_dt_bias': dt0 + jnp.log(-jnp.expm1(-dt0)),
        'dn_norm_g': 1.0 + nrm(ks[15], (L, DN_DV), 0.01),
        'w_branch': nrm(ks[16], (L, N_BRANCH, BRANCH_WIDTH, D_MODEL), DEEPNORM_BETA * BRANCH_WIDTH ** -0.5),
        'b_gate': nrm(ks[17], (L, N_BRANCH, D_MODEL), 0.01),
        'w_out': nrm(ks[18], (L, D_MODEL, D_MODEL), DEEPNORM_BETA * D_MODEL ** -0.5),
        'ln1_g': 1.0 + nrm(ks[19], (L, D_MODEL), 0.01),
        'ln1_b': nrm(ks[20], (L, D_MODEL), 0.01),
        'mlp_w1': nrm(ks[21], (L, D_MODEL, D_FF), DEEPNORM_BETA * D_MODEL ** -0.5),
        'mlp_b1': nrm(ks[22], (L, D_FF), 0.01),
        'mlp_w2': nrm(ks[23], (L, D_FF, D_MODEL), DEEPNORM_BETA * D_FF ** -0.5),
        'mlp_b2': nrm(ks[24], (L, D_MODEL), 0.01),
        'ln2_g': 1.0 + nrm(ks[25], (L, D_MODEL), 0.01),
        'ln2_b': nrm(ks[26], (L, D_MODEL), 0.01),
    }


def reference(x, w_in, lru_conv_w, lru_conv_b, lru_wa, lru_ba, lru_wi, lru_bi, lru_lambda,
              gla_wa2, gla_ba2, gla_norm_g, dn_conv_w, dn_a_log, dn_dt_bias, dn_norm_g,
              w_branch, b_gate, w_out, ln1_g, ln1_b, mlp_w1, mlp_b1, mlp_w2, mlp_b2,
              ln2_g, ln2_b):
    for l in range(DEPTH):
        m = _token_mixer(x, w_in[l], lru_conv_w[l], lru_conv_b[l], lru_wa[l], lru_ba[l], lru_wi[l],
                         lru_bi[l], lru_lambda[l], gla_wa2[l], gla_ba2[l], gla_norm_g[l],
                         dn_conv_w[l], dn_a_log[l], dn_dt_bias[l], dn_norm_g[l],
                         w_branch[l], b_gate[l], w_out[l])
        x = _layernorm(DEEPNORM_ALPHA * x + m, ln1_g[l], ln1_b[l])
        h = _squared_relu_mlp(x, mlp_w1[l], mlp_b1[l], mlp_w2[l], mlp_b2[l])
        x = _layernorm(DEEPNORM_ALPHA * x + h, ln2_g[l], ln2_b[l])
    return x
```

```python
import numpy as np
import concourse.bass as bass
import concourse.mybir as mybir
from concourse.bass_utils import run_bass_kernel_spmd

F32 = mybir.dt.float32
BF16 = mybir.dt.bfloat16
AF = mybir.ActivationFunctionType
ALU = mybir.AluOpType
AX = mybir.AxisListType

ENGS = ("pe", "act", "dve", "pool", "sp")


class Buf:
    __slots__ = ("name", "w", "r", "dsem", "dexp", "dreads")

    def __init__(self, name, dsem=None):
        self.name = name
        self.w = None
        self.r = {}
        self.dsem = dsem
        self.dexp = 0
        self.dreads = []


class Prog:
    def __init__(self, nc, stack):
        self.nc = nc
        self.stack = stack
        self.q = {e: [] for e in ENGS}
        self.cnt = {e: 0 for e in ENGS}
        self.sem = {e: stack.enter_context(nc.semaphore("s_" + e)) for e in ENGS if e != "sp"}
        self.seen = {e: {} for e in ENGS}
        self.nsem = 0
        self.n_wait = 0

    def new_sem(self, name):
        self.nsem += 1
        return self.stack.enter_context(self.nc.semaphore(name))

    def buf(self, name, dma=False):
        return Buf(name, self.new_sem("d_" + name) if dma else None)

    def _need(self, eng, waits, sem, val, key):
        if val <= 0:
            return
        if self.seen[eng].get(key, 0) >= val:
            return
        self.seen[eng][key] = val
        waits.append((sem, val))

    def _deps(self, eng, reads, writes):
        waits = []
        for b in reads:
            if b.w is not None:
                we, wi = b.w
                if not (we == "pe" and eng == "pe"):
                    self._need(eng, waits, self.sem[we], wi, we)
                elif False:
                    pass
            if b.dsem is not None and b.dexp > 0:
                self._need(eng, waits, b.dsem, b.dexp, id(b.dsem))
        for b in writes:
            if b.w is not None:
                we, wi = b.w
                if not (we == "pe" and eng == "pe"):
                    self._need(eng, waits, self.sem[we], wi, we)
            for re_, ri in b.r.items():
                if not (re_ == "pe" and eng == "pe"):
                    self._need(eng, waits, self.sem[re_], ri, re_)
            if b.dsem is not None and b.dexp > 0:
                self._need(eng, waits, b.dsem, b.dexp, id(b.dsem))
            for (s, v) in b.dreads:
                self._need(eng, waits, s, v, id(s))
        return waits

    def op(self, eng, fn, reads=(), writes=()):
        waits = self._deps(eng, reads, writes)
        self.cnt[eng] += 1
        idx = self.cnt[eng]
        self.q[eng].append((fn, waits, (self.sem[eng], 1)))
        self.seen[eng][eng] = max(self.seen[eng].get(eng, 0), 0)
        for b in reads:
            b.r[eng] = idx
        for b in writes:
            b.w = (eng, idx)
            b.r = {}
            b.dreads = []
        return idx

    def dma_in(self, qeng, fn, dst, reads=()):
        assert dst.dsem is not None
        waits = self._deps(qeng, reads, [dst])
        dst.dexp += 16
        self.q[qeng].append((fn, waits, (dst.dsem, 16)))
        dst.w = None
        dst.r = {}
        dst.dreads = []

    def dma_out(self, qeng, fn, src, osem_state):
        waits = self._deps(qeng, [src], [])
        osem_state[1] += 16
        self.q[qeng].append((fn, waits, (osem_state[0], 16)))
        src.dreads.append((osem_state[0], osem_state[1]))

    def wait_all(self, eng, sem_state):
        self.q[eng].append((None, [(sem_state[0], sem_state[1])], None))

    def emit(self):
        nc = self.nc
        engobj = {"pe": "tensor", "act": "scalar", "dve": "vector", "pool": "gpsimd", "sp": "sync"}
        with nc.Block() as block:
            for e in ENGS:
                items = self.q[e]
                if not items:
                    continue

                def body(eng, items=items):
                    for (fn, waits, inc) in items:
                        for (s, v) in waits:
                            eng.wait_ge(s, v)
                            self.n_wait += 1
                        if fn is None:
                            continue
                        ins = fn(eng)
                        if inc is not None:
                            ins.then_inc(inc[0], inc[1])

                getattr(block, engobj[e])(body)

    def barrier(self):
        for e in ("pe", "act", "dve"):
            waits = []
            for e2 in ("pe", "act", "dve"):
                if e2 != e:
                    self._need(e, waits, self.sem[e2], self.cnt[e2], e2)
            if waits:
                self.q[e].append((None, waits, None))


D = 2048
KC = 16
T = 512
DIN = 15392
ALPHA = (2.0 * 4) ** 0.25
LN_EPS = 1e-5
NORM_EPS = 1e-6
C_LRUX, C_LRUY, C_Q, C_K, C_V, C_ALR, C_R = 0, 1024, 2048, 2560, 3072, 4096, 4112
C_DQKV, C_DB, C_DA, C_DZ, C_GATE = 5136, 8208, 8216, 8224, 9248
NSLOT = 6

PV = {}
_o = 0
for _n, _w in [("lcw", 32), ("lcb", 8), ("lba", 8), ("lbi", 8), ("llam", 8), ("glag", 2), ("dcw", 96), ("dng", 1),
               ("bg", 48), ("ln1g", 16), ("ln1b", 16), ("ln2g", 16), ("ln2b", 16), ("b2", 16), ("b1", 64),
               ("alog", 8), ("dtb", 8), ("ba2", 512), ("wa2", 512)]:
    PV[_n] = (_o, _w)
    _o += _w
NPV = _o


def unit_list():
    u = [("lrug",)]
    for c in range(8):
        u += [("fm", C_LRUX + c * 128), ("fm", C_LRUY + c * 128)]
    u += [("misc",)]
    for j in range(4):
        u += [("tm", C_K, j)]
    for g in range(2):
        for j in range(4):
            u += [("tm", C_V + g * 512, j)]
    for h in range(4):
        u += [("fm", C_Q + h * 128), ("fm", C_K + h * 128), ("fm", C_R + h * 256), ("fm", C_R + h * 256 + 128)]
    for h in range(8):
        u += [("fm", C_DQKV + h * 128), ("fm", C_DQKV + 1024 + h * 128), ("fm", C_DQKV + 2048 + h * 128), ("fm", C_DZ + h * 128)]
    for m in range(16):
        for b in range(3):
            u += [("fm", C_GATE + b * 2048 + m * 128)]
        u += [("brA", m), ("brB", m)]
    for m in range(16):
        u += [("out", m)]
    for f in range(64):
        u += [("w1", f)]
    for m in range(16):
        for j in range(4):
            u += [("w2", m, j)]
    return u


UNITS = unit_list()
NU = len(UNITS)


def pack_layer_units(w_in, lru_wa, lru_wi, w_branch, w_out, w1, w2):
    out = np.zeros((NU, 128, 2048), np.float32)

    def fm(W, c0, kc):
        return W[:, c0:c0 + 128].reshape(kc, 128, 128).transpose(1, 0, 2).reshape(128, kc * 128)

    for i, tag in enumerate(UNITS):
        k = tag[0]
        if k == "fm":
            out[i] = fm(w_in, tag[1], 16)
        elif k == "tm":
            c0, j = tag[1], tag[2]
            blk = w_in[:, c0:c0 + 512].reshape(16, 128, 512)[4 * j:4 * j + 4]
            out[i] = blk.transpose(1, 0, 2).reshape(128, 2048)
        elif k == "misc":
            blk = np.concatenate([w_in[:, C_ALR:C_ALR + 16], w_in[:, C_DB:C_DB + 16]], axis=1)
            out[i, :, :512] = blk.reshape(16, 128, 32).transpose(1, 0, 2).reshape(128, 512)
        elif k == "lrug":
            bd = np.zeros((128, 8, 2, 128), np.float32)
            for c in range(8):
                for gi, wg in enumerate((lru_wa, lru_wi)):
                    bd[0:64, c, gi, 0:64] = wg[2 * c]
                    bd[64:128, c, gi, 64:128] = wg[2 * c + 1]
            out[i] = bd.reshape(128, 2048)
        elif k == "brA":
            m = tag[1]
            out[i, :, 0:1024] = fm(w_branch[0], m * 128, 8)
            out[i, :, 1024:2048] = fm(w_branch[1], m * 128, 8)
        elif k == "brB":
            m = tag[1]
            out[i, :, 0:1024] = fm(w_branch[2], m * 128, 8)
        elif k == "out":
            out[i] = fm(w_out, tag[1] * 128, 16)
        elif k == "w1":
            out[i] = fm(w1, tag[1] * 128, 16)
        elif k == "w2":
            m, j = tag[1], tag[2]
            blk = w2[:, m * 128:(m + 1) * 128].reshape(64, 128, 128)[16 * j:16 * j + 16]
            out[i] = blk.transpose(1, 0, 2).reshape(128, 2048)
    return out


def pack_pv(lcw, lcb, lba, lbi, llam, wa2, ba2, glag, dcw, alog, dtb, dng, bgate, ln1g, ln1b, b1, b2, ln2g, ln2b):
    pv = np.zeros((128, NPV), np.float32)

    def put(name, arr):
        o, w = PV[name]
        pv[:, o:o + w] = arr

    def fmv(v, nch):
        return v.reshape(nch, 128).T

    put("lcw", lcw.reshape(4, 8, 128).transpose(2, 1, 0).reshape(128, 32))
    put("lcb", fmv(lcb, 8)); put("lba", fmv(lba, 8)); put("lbi", fmv(lbi, 8)); put("llam", fmv(llam, 8))
    put("glag", fmv(glag, 2))
    put("dcw", dcw.reshape(4, 24, 128).transpose(2, 1, 0).reshape(128, 96))
    put("dng", dng.reshape(128, 1))
    put("bg", bgate.reshape(3, 16, 128).transpose(2, 0, 1).reshape(128, 48))
    put("ln1g", fmv(ln1g, 16)); put("ln1b", fmv(ln1b, 16)); put("ln2g", fmv(ln2g, 16)); put("ln2b", fmv(ln2b, 16))
    put("b2", fmv(b2, 16)); put("b1", fmv(b1, 64))
    put("alog", np.broadcast_to(alog[None, :], (128, 8)))
    put("dtb", np.broadcast_to(dtb[None, :], (128, 8)))
    put("ba2", np.broadcast_to(ba2[None, :], (128, 512)))
    o, w = PV["wa2"]
    pv[0:16, o:o + w] = wa2
    return pv


class Tl:
    __slots__ = ("ap", "b")

    def __init__(self, ap, b):
        self.ap = ap
        self.b = b


def build_program(ntiles, L, dbg=False, stop_after=None):
    from contextlib import ExitStack
    nc = bass.Bass("TRN2", target_bir_lowering=False)
    xT_d = nc.dram_tensor("xT", [ntiles, 128, KC, T], F32, kind="ExternalInput").ap()
    W_d = nc.dram_tensor("W", [L, NU, 128, 2048], F32, kind="ExternalInput").ap()
    pv_d = nc.dram_tensor("pv", [L, 128, NPV], F32, kind="ExternalInput").ap()
    oT_d = nc.dram_tensor("oT", [ntiles, 128, KC, T], F32, kind="ExternalOutput").ap()
    NS = L * (8 + 24 + 72 + 1024 + 1024)
    sti_d = nc.dram_tensor("st_in", [128, NS], F32, kind="ExternalInput").ap()
    sto_d = nc.dram_tensor("st_out", [128, NS], F32, kind="ExternalOutput").ap()
    if dbg:
        dbg_d = nc.dram_tensor("dbg", [3, 128, 8, T], BF16, kind="ExternalOutput").ap()
    st = ExitStack()
    with st:
        P = Prog(nc, st)

        def sbt(name, shape, dt=F32):
            return st.enter_context(nc.sbuf_tensor(name, shape, dt))

        xr_t = sbt("xr", [128, KC, T]); xb_t = sbt("xb", [128, KC, T], BF16)
        xr = [Tl(xr_t[:, k, :], P.buf("xr%d" % k, dma=True)) for k in range(KC)]
        xb = [Tl(xb_t[:, k, :], P.buf("xb%d" % k)) for k in range(KC)]
        ring_t = sbt("ring", [128, NSLOT, 2048], BF16)
        ring = [Tl(ring_t[:, s, :], P.buf("ring%d" % s, dma=True)) for s in range(NSLOT)]
        pv_t = sbt("pvt", [128, 2, NPV])
        pvb = [P.buf("pv%d" % i, dma=True) for i in range(2)]
        lrug_t = sbt("lrug", [128, 2048], BF16); lrug = Tl(lrug_t[:], P.buf("lrug"))
        misc_t = sbt("miscp", [128, 512], BF16); miscp = Tl(misc_t[:], P.buf("miscp"))
        lru_h_t = sbt("lru_h", [128, L, 8]); lru_tail_t = sbt("lru_tail", [128, L, 8, 3])
        dn_tail_t = sbt("dn_tail", [128, L, 24, 3])
        glaS_t = sbt("glaS", [128, L, 4, 256]); dnS_t = sbt("dnS", [128, L, 8, 128])
        glaSb_t = sbt("glaSb", [128, 4, 256], BF16); dnSb_t = sbt("dnSb", [128, 8, 128], BF16)
        bstate = P.buf("state")
        bglaS = [[P.buf("glaS") for h in range(4)] for l in range(L)]
        bglaSb = [P.buf("glaSb") for h in range(4)]
        bdnS = [[P.buf("dnS") for g in range(2)] for l in range(L)]
        bdnSb = [P.buf("dnSb") for g in range(2)]
        ident_t = sbt("ident", [128, 128]); ones_f_t = sbt("ones_f", [128, 128]); nones_f_t = sbt("nones_f", [128, 128])
        ones_b_t = sbt("ones_b", [128, 128], BF16)
        LS_t = sbt("LS", [128, 128]); UI_t = sbt("UI", [128, 128])
        LSg_t = sbt("LSg", [128, 128]); UIg_t = sbt("UIg", [128, 128])
        bconst = P.buf("const")
        arena_t = sbt("arena", [128, 19968])
        NA = 19968
        ps = []
        for i in range(8):
            pt = st.enter_context(nc.psum_tensor("ps%d" % i, [128, 512], F32))
            ps.append(Tl(pt[:], P.buf("ps%d" % i)))
        osem = [P.new_sem("osem"), 0]

        def bl(x):
            return [t.b if isinstance(t, Tl) else t for t in x]

        def act(out, in_, func, R, W, bias=0.0, scale=1.0):
            P.op("act", lambda e: e.activation(out=out, in_=in_, func=func, bias=bias, scale=scale), bl(R), bl(W))

        def tt(out, a, b, op, R, W):
            P.op("dve", lambda e: e.tensor_tensor(out=out, in0=a, in1=b, op=op), bl(R), bl(W))

        def ts(out, a, s1, op0, R, W, s2=None, op1=None):
            if op1 is None:
                P.op("dve", lambda e: e.tensor_scalar(out=out, in0=a, scalar1=s1, scalar2=None, op0=op0), bl(R), bl(W))
            else:
                P.op("dve", lambda e: e.tensor_scalar(out=out, in0=a, scalar1=s1, scalar2=s2, op0=op0, op1=op1), bl(R), bl(W))

        def stt(out, a, s, b, op0, op1, R, W):
            P.op("dve", lambda e: e.scalar_tensor_tensor(out=out, in0=a, scalar=s, in1=b, op0=op0, op1=op1), bl(R), bl(W))

        def cp(out, in_, R, W, eng="dve"):
            if eng == "act":
                act(out, in_, AF.Copy, R, W)
            else:
                P.op("dve", lambda e: e.tensor_copy(out=out, in_=in_), bl(R), bl(W))

        def mms(lst, R, W):
            def fn(e):
                ins = None
                for (o, l_, r_, s0, s1) in lst:
                    ins = e.matmul(o, lhsT=l_, rhs=r_, start=s0, stop=s1)
                return ins
            P.op("pe", fn, bl(R), bl(W))

        def transp(out, in_, R, W):
            P.op("pe", lambda e: e.transpose(out, in_, ident_t[:]), bl(R) + [bconst], bl(W))

        class Arena:
            def __init__(self, base, limit=NA):
                self.o = base; self.limit = limit
            def f32(self, n, name="t"):
                a = arena_t[:, self.o:self.o + n]; self.o += n
                assert self.o <= self.limit, "arena overflow"
                return Tl(a, P.buf(name))
            def bf16(self, n, name="t"):
                n2 = (n + 1) // 2
                a = arena_t[:, self.o:self.o + n2].bitcast(BF16); self.o += n2
                assert self.o <= self.limit, "arena overflow"
                return Tl(a, P.buf(name))

        stream = [(l, u) for _t in range(ntiles) for l in range(L) for u in range(NU)]
        wstate = {"issued": 0, "next": 0}

        def w_issue():
            i = wstate["issued"]
            if i >= len(stream):
                return
            l, u = stream[i]
            slot = ring[i % NSLOT]
            P.dma_in("pool", lambda e, l=l, u=u, slot=slot: e.dma_start(out=slot.ap, in_=W_d[l, u]), slot.b)
            wstate["issued"] += 1

        def wget(l, tag):
            i = wstate["next"]
            assert stream[i][0] == l and UNITS[stream[i][1]] == tag, (stream[i], UNITS[stream[i][1]], tag)
            while wstate["issued"] < min(len(stream), i + NSLOT - 3):
                w_issue()
            wstate["next"] += 1
            return ring[i % NSLOT]

        def proj16(pst, u, M=128, col0=0, stride=128):
            lst = [(pst.ap[0:M, :], u.ap[:, kc * stride + col0: kc * stride + col0 + M], xb[kc].ap, kc == 0, kc == KC - 1)
                   for kc in range(KC)]
            mms(lst, [u] + xb, [pst])

        def pool_op(fn, R, W):
            P.op("pool", fn, bl(R), bl(W))

        pool_op(lambda e: e.memset(ident_t[:], 0.0), [], [bconst])
        pool_op(lambda e: e.affine_select(out=ident_t[:], in_=ident_t[:], pattern=[[-1, 128]], compare_op=ALU.not_equal, fill=1.0, base=0, channel_multiplier=1), [bconst], [bconst])
        pool_op(lambda e: e.memset(ones_f_t[:], 1.0), [], [bconst])
        pool_op(lambda e: e.memset(nones_f_t[:], -1.0), [], [bconst])
        pool_op(lambda e: e.memset(ones_b_t[:], 1.0), [], [bconst])
        for (mt, cmpop, lower) in [(LS_t, ALU.is_gt, True), (UI_t, ALU.is_ge, False)]:
            pool_op(lambda e, mt=mt: e.memset(mt[:], 0.0), [], [bconst])
            for b_ in range(2):
                blk = mt[64 * b_:64 * b_ + 64, 64 * b_:64 * b_ + 64]
                pool_op(lambda e, blk=blk: e.memset(blk, 1.0), [bconst], [bconst])
                pat = [[-1, 64]] if lower else [[1, 64]]
                cm = 1 if lower else -1
                pool_op(lambda e, blk=blk, pat=pat, cm=cm, cmpop=cmpop: e.affine_select(out=blk, in_=blk, pattern=pat, compare_op=cmpop, fill=0.0, base=0, channel_multiplier=cm), [bconst], [bconst])
        pool_op(lambda e: e.tensor_scalar(out=LSg_t[:], in0=LS_t[:], scalar1=-1.0 / 16.0, scalar2=None, op0=ALU.mult), [bconst], [bconst])
        pool_op(lambda e: e.tensor_scalar(out=UIg_t[:], in0=UI_t[:], scalar1=-1.0 / 16.0, scalar2=None, op0=ALU.mult), [bconst], [bconst])
        bload = P.buf("stload", dma=True)
        st_views = [(lru_h_t[:].rearrange("p l c -> p (l c)"), L * 8), (lru_tail_t[:].rearrange("p l c k -> p (l c k)"), L * 24),
                    (dn_tail_t[:].rearrange("p l c k -> p (l c k)"), L * 72), (glaS_t[:].rearrange("p l h v -> p (l h v)"), L * 1024),
                    (dnS_t[:].rearrange("p l h v -> p (l h v)"), L * 1024)]
        _o = 0
        st_off = []
        for (v_, n_) in st_views:
            st_off.append(_o)
            P.dma_in("sp", lambda e, v_=v_, o_=_o, n_=n_: e.dma_start(out=v_, in_=sti_d[:, o_:o_ + n_]), bload)
            _o += n_
        P.op("dve", lambda e: e.tensor_copy(out=lru_h_t[:, 0, 0:1], in_=lru_h_t[:, 0, 0:1]), [bload], [bstate])
        pool_op(lambda e: e.memset(glaSb_t[:], 0.0), [], bglaSb)
        pool_op(lambda e: e.memset(dnSb_t[:], 0.0), [], bdnSb)
        for l in range(L):
            for h in range(4):
                bglaS[l][h].w = bstate.w
            for g in range(2):
                bdnS[l][g].w = bstate.w

        def pvc(slot, name, j=0, n=1):
            o, w = PV[name]
            return pv_t[:, slot, o + j:o + j + n]

        def layernorm(A, pslot, gname, bname, t1s=None):
            s1, s2 = ps[6], ps[7]
            lnset = [(A.bf16(T, "lnyb"), A.bf16(T, "lnyq"), (t1s[_i] if t1s is not None else A.f32(T, "lnt"))) for _i in range(2)]
            for m in range(KC):
                yb_, yq_, _ = lnset[m % 2]
                act(yb_.ap, xr[m].ap, AF.Copy, [xr[m]], [yb_])
                act(yq_.ap, xr[m].ap, AF.Square, [xr[m]], [yq_])
                mms([(s1.ap, ones_b_t[:], yb_.ap, m == 0, m == KC - 1)], [yb_, bconst], [s1])
                mms([(s2.ap, ones_b_t[:], yq_.ap, m == 0, m == KC - 1)], [yq_, bconst], [s2])
            msq = A.f32(T, "msq"); var = msq; rstd = A.f32(T, "rstd"); nmr = A.f32(T, "nmr")
            act(msq.ap, s1.ap, AF.Square, [s1], [msq], scale=1.0 / D)
            stt(var.ap, s2.ap, 1.0 / D, msq.ap, ALU.mult, ALU.subtract, [s2, msq], [msq])
            ts(var.ap, var.ap, LN_EPS, ALU.add, [var], [var])
            act(var.ap, var.ap, AF.Sqrt, [var], [var])
            P.op("dve", lambda e: e.reciprocal(out=rstd.ap, in_=var.ap), bl([var]), bl([rstd]))
            stt(nmr.ap, s1.ap, -1.0 / D, rstd.ap, ALU.mult, ALU.mult, [s1, rstd], [nmr])
            for m in range(KC):
                t1 = lnset[m % 2][2]
                tt(t1.ap, xr[m].ap, rstd.ap, ALU.mult, [xr[m], rstd], [t1])
                tt(t1.ap, t1.ap, nmr.ap, ALU.add, [t1, nmr], [t1])
                act(xr[m].ap, t1.ap, AF.Identity, [t1, pvb[pslot]], [xr[m]], bias=pvc(pslot, bname, m), scale=pvc(pslot, gname, m))
                cp(xb[m].ap, xr[m].ap, [xr[m]], [xb[m]])

        Y_LRU, Y_GLA, Y_DN = 0, 2048, 4096
        TMP0 = 6144

        def ybr(which):
            return [Tl(arena_t[:, which + c * 256: which + (c + 1) * 256].bitcast(BF16), P.buf("ybr")) for c in range(8)]

        def phase_lru(l, pslot, ylru):
            A = Arena(TMP0)
            u0 = wget(l, ("lrug",))
            cp(lrug.ap, u0.ap, [u0], [lrug])
            pb_ = pvb[pslot]
            cl = A.f32(8, "cl"); cl2 = A.f32(8, "cl2")
            act(cl.ap, pvc(pslot, "llam", 0, 8), AF.Exp, [pb_], [cl], scale=-1.0)
            act(cl.ap, cl.ap, AF.Ln, [cl], [cl], bias=1.0)
            ts(cl2.ap, cl.ap, -16.0, ALU.mult, [cl], [cl2])
            ts(cl.ap, cl.ap, -8.0, ALU.mult, [cl], [cl])
            sets = []
            for _ in range(2):
                sets.append(dict(xpad=A.f32(T + 3, "xpad"), xc=A.f32(T, "xc"), xcb=A.bf16(T, "xcb"), r=A.f32(T, "r"), ig=A.f32(T, "i"),
                                 a=A.f32(T, "a"), a2=A.f32(T, "a2"), hh=A.f32(T, "h"), y2=A.f32(T, "y2"), sg=A.f32(T, "sg")))
            for c in range(8):
                S_ = sets[c % 2]
                ua = wget(l, ("fm", C_LRUX + c * 128))
                pa = ps[c % 2]
                proj16(pa, ua)
                xpad = S_["xpad"]; xc = S_["xc"]; xcb = S_["xcb"]
                cp(xpad.ap[:, 0:3], lru_tail_t[:, l, c, :], [bstate], [xpad])
                act(xpad.ap[:, 3:T + 3], pa.ap, AF.Copy, [pa], [xpad])
                cp(lru_tail_t[:, l, c, :], xpad.ap[:, T:T + 3], [xpad], [bstate])
                act(xc.ap, xpad.ap[:, 3:T + 3], AF.Identity, [xpad, pb_], [xc], bias=pvc(pslot, "lcb", c), scale=pvc(pslot, "lcw", c * 4 + 3))
                for k in (2, 1, 0):
                    stt(xc.ap, xpad.ap[:, k:k + T], pvc(pslot, "lcw", c * 4 + k), xc.ap, ALU.mult, ALU.add, [xpad, xc, pb_], [xc])
                act(xcb.ap, xc.ap, AF.Copy, [xc], [xcb])
                pr, pi = ps[2], ps[3]
                mms([(pr.ap, lrug.ap[:, (c * 2 + 0) * 128:(c * 2 + 1) * 128], xcb.ap, True, True)], [lrug, xcb], [pr])
                mms([(pi.ap, lrug.ap[:, (c * 2 + 1) * 128:(c * 2 + 2) * 128], xcb.ap, True, True)], [lrug, xcb], [pi])
                r = S_["r"]; ig = S_["ig"]; a = S_["a"]; a2 = S_["a2"]
                act(r.ap, pr.ap, AF.Sigmoid, [pr, pb_], [r], bias=pvc(pslot, "lba", c))
                act(ig.ap, pi.ap, AF.Sigmoid, [pi, pb_], [ig], bias=pvc(pslot, "lbi", c))
                act(a.ap, r.ap, AF.Exp, [r, cl], [a], scale=cl.ap[:, c:c + 1])
                act(a2.ap, r.ap, AF.Exp, [r, cl2], [a2], scale=cl2.ap[:, c:c + 1])
                act(a2.ap, a2.ap, AF.Sqrt, [a2], [a2], bias=1.0, scale=-1.0)
                tt(ig.ap, ig.ap, xc.ap, ALU.mult, [ig, xc], [ig])
                tt(ig.ap, ig.ap, a2.ap, ALU.mult, [ig, a2], [ig])
                hh = S_["hh"]
                P.op("dve", lambda e, hh=hh, a=a, ig=ig, c=c: e.tensor_tensor_scan(out=hh.ap, data0=a.ap, data1=ig.ap, initial=lru_h_t[:, l, c:c + 1], op0=ALU.mult, op1=ALU.add), bl([a, ig, bstate]), bl([hh]))
                cp(lru_h_t[:, l, c:c + 1], hh.ap[:, T - 1:T], [hh], [bstate])
                ub = wget(l, ("fm", C_LRUY + c * 128))
                pyb = ps[4 + c % 2]
                proj16(pyb, ub)
                y2 = S_["y2"]; sg = S_["sg"]
                act(y2.ap, pyb.ap, AF.Square, [pyb], [y2])
                ts(y2.ap, y2.ap, 0.044715, ALU.mult, [y2], [y2], s2=1.0, op1=ALU.add)
                tt(y2.ap, y2.ap, pyb.ap, ALU.mult, [y2, pyb], [y2])
                act(sg.ap, y2.ap, AF.Sigmoid, [y2], [sg], scale=1.5957691216057308)
                tt(sg.ap, sg.ap, pyb.ap, ALU.mult, [sg, pyb], [sg])
                tt(ylru[c].ap, sg.ap, hh.ap, ALU.mult, [sg, hh], [ylru[c]])


        def phase_gla(l, pslot, ygla, dbgt=None):
            A = Arena(TMP0)
            pb_ = pvb[pslot]
            alrT = A.f32(T, "alrT")
            Gtok = [A.f32(512, "Gtok") for _ in range(4)]
            kd = [A.bf16(512, "kd") for _ in range(4)]
            vtok = [A.bf16(1024, "vtok") for _ in range(4)]
            E1 = A.f32(T, "E1"); E2 = A.f32(T, "E2"); dec = A.f32(8, "dec"); qe = A.bf16(T, "qe"); ke = A.bf16(T, "ke")
            attm = [A.bf16(128, "attm") for _ in range(2)]
            sq = [A.f32(T, "sq") for _ in range(2)]; rstd = A.f32(T, "rstd"); sr = A.f32(T, "sr"); t1 = A.f32(T, "t1")
            Dx = A.f32(512, "Dx"); zt = A.f32(512, "zt")
            um = wget(l, ("misc",))
            cp(miscp.ap, um.ap[:, 0:512], [um], [miscp])
            pst = ps[0]
            mms([(pst.ap[0:16, :], miscp.ap[:, kc * 32: kc * 32 + 16], xb[kc].ap, kc == 0, kc == KC - 1) for kc in range(KC)], [miscp] + xb, [pst])
            act(alrT.ap[0:16, :], pst.ap[0:16, :], AF.Copy, [pst], [alrT])
            uk = [wget(l, ("tm", C_K, j)) for j in range(4)]
            wa2 = pvc(pslot, "wa2", 0, 512)
            for tb in range(4):
                pz = ps[2]
                mms([(pz.ap, alrT.ap[0:16, tb * 128:(tb + 1) * 128], wa2[0:16, :], True, True)], [alrT, pb_], [pz])
                tt(zt.ap, pz.ap, pvc(pslot, "ba2", 0, 512), ALU.add, [pz, pb_], [zt])
                act(zt.ap, zt.ap, AF.Exp, [zt], [zt], scale=-1.0)
                act(Gtok[tb].ap, zt.ap, AF.Ln, [zt], [Gtok[tb]], bias=1.0)
                pk = ps[tb % 2]
                mms([(pk.ap, xb[kc].ap[:, tb * 128:(tb + 1) * 128], uk[kc // 4].ap[:, (kc % 4) * 512:(kc % 4 + 1) * 512], kc == 0, kc == KC - 1) for kc in range(KC)], uk + xb, [pk])
                psD = ps[3]
                mms([(psD.ap, LSg_t[:], Gtok[tb].ap, True, True)], [Gtok[tb], bconst], [psD])
                act(Dx.ap, psD.ap, AF.Exp, [psD], [Dx])
                tt(kd[tb].ap, pk.ap, Dx.ap, ALU.mult, [pk, Dx], [kd[tb]])
            for g in range(2):
                uv = [wget(l, ("tm", C_V + g * 512, j)) for j in range(4)]
                for tb in range(4):
                    pv_ = ps[4 + tb % 2]
                    mms([(pv_.ap, xb[kc].ap[:, tb * 128:(tb + 1) * 128], uv[kc // 4].ap[:, (kc % 4) * 512:(kc % 4 + 1) * 512], kc == 0, kc == KC - 1) for kc in range(KC)], uv + xb, [pv_])
                    act(vtok[tb].ap[:, g * 512:(g + 1) * 512], pv_.ap, AF.Copy, [pv_], [vtok[tb]])
            for h in range(4):
                Sv = glaS_t[:, l, h, :]
                bS = bglaS[l][h]
                act(glaSb_t[:, h, :], Sv, AF.Copy, [bS], [bglaSb[h]])
                pbt = ps[2]
                mms([(pbt.ap[:, tb * 128:(tb + 1) * 128], Gtok[tb].ap[:, h * 128:(h + 1) * 128], UIg_t[:], True, True) for tb in range(4)], Gtok + [bconst], [pbt])
                act(E1.ap, pbt.ap, AF.Exp, [pbt], [E1])
                act(E2.ap, pbt.ap, AF.Exp, [pbt], [E2], scale=-1.0)
                cp(dec.ap, E1.ap[:, 63:512:64], [E1], [dec])
                uq = wget(l, ("fm", C_Q + h * 128))
                pq = ps[0]
                proj16(pq, uq)
                stt(qe.ap, pq.ap, 128.0 ** -0.5, E1.ap, ALU.mult, ALU.mult, [pq, E1], [qe])
                uk_ = wget(l, ("fm", C_K + h * 128))
                pk = ps[1]
                proj16(pk, uk_)
                tt(ke.ap, pk.ap, E2.ap, ALU.mult, [pk, E2], [ke])
                po = [ps[4], ps[5]]
                for tb in range(4):
                    pa = ps[3]
                    tsl = slice(tb * 128, (tb + 1) * 128)
                    mms([(pa.ap[:, 0:128], ke.ap[:, tsl], qe.ap[:, tsl], True, True)], [ke, qe], [pa])
                    am = attm[tb % 2]
                    tt(am.ap, pa.ap[:, 0:128], UI_t[:], ALU.mult, [pa, bconst], [am])
                    mms([(po[vc].ap[:, tsl], vtok[tb].ap[:, h * 256 + vc * 128: h * 256 + (vc + 1) * 128], am.ap, True, False) for vc in range(2)], [vtok[tb], am], po)
                    for half in range(2):
                        n = 2 * tb + half
                        r0 = 64 * half
                        csl = slice(n * 64, (n + 1) * 64)
                        mms([(po[vc].ap[:, csl], glaSb_t[:, h, vc * 128:(vc + 1) * 128], qe.ap[:, csl], False, half == 1) for vc in range(2)], [bglaSb[h], qe], po)
                        pu = ps[6]
                        mms([(pu.ap[:, 0:256], kd[tb].ap[r0:r0 + 64, h * 128:(h + 1) * 128], vtok[tb].ap[r0:r0 + 64, h * 256:(h + 1) * 256], True, True)], [kd[tb], vtok[tb]], [pu])
                        stt(Sv, Sv, dec.ap[:, n:n + 1], pu.ap[:, 0:256], ALU.mult, ALU.add, [bS, dec, pu], [bS])
                        act(glaSb_t[:, h, :], Sv, AF.Copy, [bS], [bglaSb[h]])
                if dbgt is not None and h == 3:
                    cp(dbgt[0].ap, Gtok[0].ap, [Gtok[0]], [dbgt[0]])
                    cp(dbgt[1].ap, kd[0].ap, [kd[0]], [dbgt[1]])
                    cp(dbgt[2].ap, E1.ap, [E1], [dbgt[2]])
                    cp(dbgt[3].ap, qe.ap, [qe], [dbgt[3]])
                    cp(dbgt[4].ap, ke.ap, [ke], [dbgt[4]])
                    cp(dbgt[5].ap, vtok[0].ap[:, 0:512], [vtok[0]], [dbgt[5]])
                    cp(dbgt[6].ap, po[0].ap, [po[0]], [dbgt[6]])
                    cp(dbgt[7].ap, po[1].ap, [po[1]], [dbgt[7]])
                for vc in range(2):
                    act(sq[vc].ap, po[vc].ap, AF.Square, [po[vc]], [sq[vc]])
                pss = ps[7]
                mms([(pss.ap, ones_f_t[:], sq[0].ap, True, False), (pss.ap, ones_f_t[:], sq[1].ap, False, True)], [sq[0], sq[1], bconst], [pss])
                act(rstd.ap, pss.ap, AF.Sqrt, [pss], [rstd], bias=NORM_EPS, scale=1.0 / 256.0)
                P.op("dve", lambda e: e.reciprocal(out=rstd.ap, in_=rstd.ap), bl([rstd]), bl([rstd]))
                for vc in range(2):
                    ur = wget(l, ("fm", C_R + h * 256 + vc * 128))
                    pr = ps[vc]
                    proj16(pr, ur)
                    act(sr.ap, pr.ap, AF.Silu, [pr], [sr])
                    stt(t1.ap, po[vc].ap, pvc(pslot, "glag", vc), rstd.ap, ALU.mult, ALU.mult, [po[vc], pb_, rstd], [t1])
                    tt(ygla[h * 2 + vc].ap, t1.ap, sr.ap, ALU.mult, [t1, sr], [ygla[h * 2 + vc]])


        def phase_gdn(l, pslot, ydn):
            A = Arena(TMP0)
            pb_ = pvb[pslot]
            f = lambda n, nm="g": A.f32(n, nm)
            h16 = lambda n, nm="g": A.bf16(n, nm)
            pad = f(T + 3); cv = f(T); qn = f(T); kn = f(T); sqt = f(T); rn = f(T); Ebc = f(T)
            qnb = h16(T); knb = h16(T); qgT = h16(T); KbgT = h16(T)
            kg = [h16(128) for _ in range(4)]; Vb = [h16(128) for _ in range(4)]; TT = [h16(128) for _ in range(4)]; AqkT = [h16(128) for _ in range(4)]
            diag = f(128); GM = f(128); EA = f(128); EB = f(128); NA = f(128); NB = f(128); A2 = f(128); B2 = f(128); Rm = f(128)
            Wm = h16(128); vnew = h16(128); egl = f(8); rstd = f(T); sz = f(T); t1 = f(T)
            btok = f(32); gtok = f(32); gctok = f(32); ekg = f(32); bgc = f(32); negA = f(8)
            pt = ps[7]
            for tb in range(4):
                mms([(pt.ap[:, tb * 16:(tb + 1) * 16], xb[kc].ap[:, tb * 128:(tb + 1) * 128], miscp.ap[:, kc * 32 + 16: kc * 32 + 32], kc == 0, kc == KC - 1) for kc in range(KC)], [miscp] + xb, [pt])
            pt3 = pt.ap[:, 0:64].rearrange("p (t c) -> p t c", c=16)
            v3 = lambda tl: tl.ap.rearrange("p (t c) -> p t c", c=8)
            act(v3(btok), pt3[:, :, 0:8], AF.Sigmoid, [pt], [btok])
            for tb in range(4):
                tt(gtok.ap[:, tb * 8:(tb + 1) * 8], pt.ap[:, tb * 16 + 8: tb * 16 + 16], pvc(pslot, "dtb", 0, 8), ALU.add, [pt, pb_], [gtok])
            act(gtok.ap, gtok.ap, AF.Exp, [gtok], [gtok])
            act(gtok.ap, gtok.ap, AF.Ln, [gtok], [gtok], bias=1.0)
            act(negA.ap, pvc(pslot, "alog", 0, 8), AF.Exp, [pb_], [negA])
            for tb in range(4):
                stt(gtok.ap[:, tb * 8:(tb + 1) * 8], gtok.ap[:, tb * 8:(tb + 1) * 8], -1.0, negA.ap, ALU.mult, ALU.mult, [gtok, negA], [gtok])
            pc = ps[6]
            for tb in range(4):
                mms([(pc.ap[:, tb * 8:(tb + 1) * 8], UI_t[:], gtok.ap[:, tb * 8:(tb + 1) * 8], True, True)], [gtok, bconst], [pc])
                mms([(pc.ap[:, 32 + tb * 8: 32 + (tb + 1) * 8], LS_t[:], gtok.ap[:, tb * 8:(tb + 1) * 8], True, True)], [gtok, bconst], [pc])
            cp(gctok.ap, pc.ap[:, 0:32], [pc], [gctok])
            act(ekg.ap, pc.ap[:, 32:64], AF.Exp, [pc], [ekg])
            act(bgc.ap, pc.ap[:, 0:32], AF.Exp, [pc], [bgc])
            tt(bgc.ap, bgc.ap, btok.ap, ALU.mult, [bgc, btok], [bgc])

            def conv_silu(u, j, out):
                pp = ps[j % 2]
                proj16(pp, u)
                cp(pad.ap[:, 0:3], dn_tail_t[:, l, j, :], [bstate], [pad])
                act(pad.ap[:, 3:T + 3], pp.ap, AF.Copy, [pp], [pad])
                cp(dn_tail_t[:, l, j, :], pad.ap[:, T:T + 3], [pad], [bstate])
                ts(cv.ap, pad.ap[:, 3:T + 3], pvc(pslot, "dcw", j * 4 + 3), ALU.mult, [pad, pb_], [cv])
                for k in (2, 1, 0):
                    stt(cv.ap, pad.ap[:, k:k + T], pvc(pslot, "dcw", j * 4 + k), cv.ap, ALU.mult, ALU.add, [pad, cv, pb_], [cv])
                act(out.ap, cv.ap, AF.Silu, [cv], [out])

            def l2n(x_, scale):
                act(sqt.ap, x_.ap, AF.Square, [x_], [sqt])
                pn = ps[2]
                mms([(pn.ap, ones_f_t[:], sqt.ap, True, True)], [sqt, bconst], [pn])
                act(rn.ap, pn.ap, AF.Sqrt, [pn], [rn], bias=NORM_EPS)
                P.op("dve", lambda e: e.reciprocal(out=rn.ap, in_=rn.ap), bl([rn]), bl([rn]))
                stt(x_.ap, x_.ap, scale, rn.ap, ALU.mult, ALU.mult, [x_, rn], [x_])

            for h in range(8):
                bS = bdnS[l][h // 4]
                Sv = dnS_t[:, l, h, :]
                Sbv = dnSb_t[:, h, :]
                bSb = bdnSb[h // 4]
                act(Sbv, Sv, AF.Copy, [bS], [bSb])
                uq = wget(l, ("fm", C_DQKV + h * 128))
                conv_silu(uq, h, qn)
                l2n(qn, 128.0 ** -0.5)
                uk_ = wget(l, ("fm", C_DQKV + 1024 + h * 128))
                conv_silu(uk_, 8 + h, kn)
                l2n(kn, 1.0)
                cp(qnb.ap, qn.ap, [qn], [qnb]); cp(knb.ap, kn.ap, [kn], [knb], eng="act")
                pe_ = ps[3]; pbb = ps[4]
                for tb in range(4):
                    ts(diag.ap, ident_t[:], gctok.ap[:, tb * 8 + h: tb * 8 + h + 1], ALU.mult, [gctok, bconst], [diag])
                    mms([(pe_.ap[:, tb * 128:(tb + 1) * 128], ones_f_t[:], diag.ap, True, True)], [diag, bconst], [pe_])
                    ts(GM.ap, ident_t[:], bgc.ap[:, tb * 8 + h: tb * 8 + h + 1], ALU.mult, [bgc, bconst], [GM])
                    mms([(pbb.ap[:, tb * 128:(tb + 1) * 128], ones_f_t[:], GM.ap, True, True)], [GM, bconst], [pbb])
                act(Ebc.ap, pe_.ap, AF.Exp, [pe_], [Ebc])
                cp(egl.ap, Ebc.ap[:, 63:512:64], [Ebc], [egl])
                tt(qgT.ap, qn.ap, Ebc.ap, ALU.mult, [qn, Ebc], [qgT])
                tt(KbgT.ap, kn.ap, pbb.ap, ALU.mult, [kn, pbb], [KbgT])
                ptk = ps[5]
                for tb in range(4):
                    transp(ptk.ap[:, tb * 128:(tb + 1) * 128], kn.ap[:, tb * 128:(tb + 1) * 128], [kn], [ptk])
                for tb in range(4):
                    ts(kg[tb].ap, ptk.ap[:, tb * 128:(tb + 1) * 128], ekg.ap[:, tb * 8 + h: tb * 8 + h + 1], ALU.mult, [ptk, ekg], [kg[tb]])
                uv = wget(l, ("fm", C_DQKV + 2048 + h * 128))
                conv_silu(uv, 16 + h, sqt)
                for tb in range(4):
                    transp(ptk.ap[:, tb * 128:(tb + 1) * 128], sqt.ap[:, tb * 128:(tb + 1) * 128], [sqt], [ptk])
                for tb in range(4):
                    ts(Vb[tb].ap, ptk.ap[:, tb * 128:(tb + 1) * 128], btok.ap[:, tb * 8 + h: tb * 8 + h + 1], ALU.mult, [ptk, btok], [Vb[tb]])
                for tb in range(4):
                    tsl = slice(tb * 128, (tb + 1) * 128)
                    ts(GM.ap, UI_t[:], gtok.ap[:, tb * 8 + h: tb * 8 + h + 1], ALU.mult, [gtok, bconst], [GM])
                    pd = ps[2]
                    mms([(pd.ap[:, 0:128], GM.ap, ones_f_t[:], True, False), (pd.ap[:, 0:128], nones_f_t[:], GM.ap, False, True)], [GM, bconst], [pd])
                    ts(EA.ap, pd.ap[:, 0:128], 0.0, ALU.min, [pd], [EA])
                    act(EA.ap, EA.ap, AF.Exp, [EA], [EA])
                    ts(EB.ap, pd.ap[:, 0:128], -1.0, ALU.mult, [pd], [EB], s2=0.0, op1=ALU.min)
                    act(EB.ap, EB.ap, AF.Exp, [EB], [EB])
                    pg = ps[3]
                    mms([(pg.ap[:, 0:128], knb.ap[:, tsl], knb.ap[:, tsl], True, True)], [knb], [pg])
                    tt(EA.ap, EA.ap, LS_t[:], ALU.mult, [EA, bconst], [EA])
                    stt(NA.ap, pg.ap[:, 0:128], btok.ap[:, tb * 8 + h: tb * 8 + h + 1], EA.ap, ALU.mult, ALU.mult, [pg, btok, EA], [NA])
                    pq = ps[4]
                    mms([(pq.ap[:, 0:128], knb.ap[:, tsl], qnb.ap[:, tsl], True, True)], [knb, qnb], [pq])
                    tt(EB.ap, EB.ap, UI_t[:], ALU.mult, [EB, bconst], [EB])
                    tt(AqkT[tb].ap, pq.ap[:, 0:128], EB.ap, ALU.mult, [pq, EB], [AqkT[tb]])
                    pn_ = ps[5]
                    transp(pn_.ap[:, 0:128], NA.ap, [NA], [pn_])
                    cp(NB.ap, pn_.ap[:, 0:128], [pn_], [NB], eng="act")
                    tt(Rm.ap, ident_t[:], NB.ap, ALU.subtract, [NB, bconst], [Rm])
                    Ac, Bc, An, Bn = NA, NB, A2, B2
                    for lvl in range(5):
                        pA = ps[2]; pB = ps[3]; pR = ps[4]
                        mms([(pA.ap[:, 0:128], Bc.ap, Ac.ap, True, True)], [Ac, Bc], [pA])
                        if lvl < 4:
                            mms([(pB.ap[:, 0:128], Ac.ap, Bc.ap, True, True)], [Ac, Bc], [pB])
                        cp(An.ap, pA.ap[:, 0:128], [pA], [An])
                        if lvl < 4:
                            cp(Bn.ap, pB.ap[:, 0:128], [pB], [Bn], eng="act")
                        mms([(pR.ap[:, 0:128], An.ap, Rm.ap, True, True)], [An, Rm], [pR])
                        tt(Rm.ap, Rm.ap, pR.ap[:, 0:128], ALU.add, [Rm, pR], [Rm])
                        Ac, Bc, An, Bn = An, Bn, Ac, Bc
                    cp(TT[tb].ap, Rm.ap, [Rm], [TT[tb]], eng="act")
                po = ps[6]
                for n in range(8):
                    tb = n // 2; r0 = 64 * (n % 2)
                    csl = slice(n * 64, (n + 1) * 64); rs = slice(r0, r0 + 64)
                    p1 = ps[2]; p2 = ps[3]; p3 = ps[4]
                    mms([(p1.ap[rs, 0:128], KbgT.ap[:, csl], Sbv, True, True)], [KbgT, bSb], [p1])
                    tt(Wm.ap[rs, :], Vb[tb].ap[rs, :], p1.ap[rs, 0:128], ALU.subtract, [Vb[tb], p1], [Wm])
                    mms([(p2.ap[rs, 0:128], TT[tb].ap[rs, rs], Wm.ap[rs, :], True, True)], [TT[tb], Wm], [p2])
                    act(vnew.ap[rs, :], p2.ap[rs, 0:128], AF.Copy, [p2], [vnew])
                    mms([(po.ap[:, csl], Sbv, qgT.ap[:, csl], True, False), (po.ap[:, csl], vnew.ap[rs, :], AqkT[tb].ap[rs, rs], False, True)], [bSb, qgT, vnew, AqkT[tb]], [po])
                    mms([(p3.ap[:, 0:128], kg[tb].ap[rs, :], vnew.ap[rs, :], True, True)], [kg[tb], vnew], [p3])
                    stt(Sv, Sv, egl.ap[:, n:n + 1], p3.ap[:, 0:128], ALU.mult, ALU.add, [bS, egl, p3], [bS])
                    act(Sbv, Sv, AF.Copy, [bS], [bSb])
                act(cv.ap, po.ap, AF.Square, [po], [cv])
                pss = ps[7]
                mms([(pss.ap, ones_f_t[:], cv.ap, True, True)], [cv, bconst], [pss])
                act(rstd.ap, pss.ap, AF.Sqrt, [pss], [rstd], bias=NORM_EPS, scale=1.0 / 128.0)
                P.op("dve", lambda e: e.reciprocal(out=rstd.ap, in_=rstd.ap), bl([rstd]), bl([rstd]))
                uz = wget(l, ("fm", C_DZ + h * 128))
                pz = ps[h % 2]
                proj16(pz, uz)
                act(sz.ap, pz.ap, AF.Silu, [pz], [sz])
                stt(t1.ap, po.ap, pvc(pslot, "dng", 0), rstd.ap, ALU.mult, ALU.mult, [po, pb_, rstd], [t1])
                tt(ydn[h].ap, t1.ap, sz.ap, ALU.mult, [t1, sz], [ydn[h]])


        def phase_merge(l, pslot, ylru, ygla, ydn):
            A = Arena(TMP0)
            pb_ = pvb[pslot]
            merged = [A.bf16(T, "mg") for _ in range(16)]
            gsets = [[A.f32(T, "gt") for _ in range(3)] for _ in range(2)]
            acc = [A.f32(T, "acc") for _ in range(2)]; tmp = [A.f32(T, "tmp") for _ in range(2)]
            ys = (ylru, ygla, ydn)
            for m in range(16):
                gt = gsets[m % 2]
                for b in range(3):
                    ug = wget(l, ("fm", C_GATE + b * 2048 + m * 128))
                    pg = ps[b % 2]
                    proj16(pg, ug)
                    act(gt[b].ap, pg.ap, AF.Sigmoid, [pg, pb_], [gt[b]], bias=pvc(pslot, "bg", b * 16 + m))
                uA = wget(l, ("brA", m)); uB = wget(l, ("brB", m))
                ac = acc[m % 2]; tm = tmp[m % 2]
                for b in range(3):
                    u_, off = (uA, b * 1024) if b < 2 else (uB, 0)
                    pbr = ps[2 + b]
                    mms([(pbr.ap, u_.ap[:, off + cc * 128: off + (cc + 1) * 128], ys[b][cc].ap, cc == 0, cc == 7) for cc in range(8)], [u_] + ys[b], [pbr])
                    if b == 0:
                        tt(ac.ap, gt[0].ap, pbr.ap, ALU.mult, [gt[0], pbr], [ac])
                    else:
                        tt(tm.ap, gt[b].ap, pbr.ap, ALU.mult, [gt[b], pbr], [tm])
                        if b == 1:
                            tt(ac.ap, ac.ap, tm.ap, ALU.add, [ac, tm], [ac])
                        else:
                            tt(merged[m].ap, ac.ap, tm.ap, ALU.add, [ac, tm], [merged[m]])
            for m in range(16):
                uo = wget(l, ("out", m))
                po = ps[m % 2]
                mms([(po.ap, uo.ap[:, kc * 128:(kc + 1) * 128], merged[kc].ap, kc == 0, kc == KC - 1) for kc in range(KC)], [uo] + merged, [po])
                stt(xr[m].ap, xr[m].ap, ALPHA, po.ap, ALU.mult, ALU.add, [xr[m], po], [xr[m]])
            layernorm(A, pslot, "ln1g", "ln1b")

        def phase_mlp(l, pslot):
            A = Arena(0)
            pb_ = pvb[pslot]
            hh = [A.bf16(T, "hid") for _ in range(64)]
            rl = [A.f32(T, "rl") for _ in range(2)]
            for f_ in range(64):
                u1 = wget(l, ("w1", f_))
                p1 = ps[f_ % 2]
                proj16(p1, u1)
                r_ = rl[f_ % 2]
                act(r_.ap, p1.ap, AF.Relu, [p1, pb_], [r_], bias=pvc(pslot, "b1", f_))
                tt(hh[f_].ap, r_.ap, r_.ap, ALU.mult, [r_], [hh[f_]])
            for m in range(16):
                us = [wget(l, ("w2", m, j)) for j in range(4)]
                p2 = ps[2 + m % 2]
                mms([(p2.ap, us[f_ // 16].ap[:, (f_ % 16) * 128:(f_ % 16 + 1) * 128], hh[f_].ap, f_ == 0, f_ == 63) for f_ in range(64)], us + hh, [p2])
                r_ = rl[m % 2]
                act(r_.ap, p2.ap, AF.Identity, [p2, pb_], [r_], bias=pvc(pslot, "b2", m))
                stt(xr[m].ap, xr[m].ap, ALPHA, r_.ap, ALU.mult, ALU.add, [xr[m], r_], [xr[m]])
            layernorm(A, pslot, "ln2g", "ln2b", t1s=rl)

        PHASES = {}

        def main_loop(stop_after=None):
            npv = 0
            P.dma_in("sp", lambda e: e.dma_start(out=pv_t[:, 0, :], in_=pv_d[0]), pvb[0])
            for ti in range(ntiles):
                for k in range(KC):
                    P.dma_in("sp", lambda e, ti=ti, k=k: e.dma_start(out=xr[k].ap, in_=xT_d[ti, :, k, :]), xr[k].b)
                for k in range(KC):
                    cp(xb[k].ap, xr[k].ap, [xr[k]], [xb[k]], eng=("act" if k % 2 else "dve"))
                for l in range(L):
                    pslot = npv % 2
                    npv += 1
                    if not (ti == ntiles - 1 and l == L - 1):
                        nl = (l + 1) % L
                        P.dma_in("sp", lambda e, nl=nl, s_=npv % 2: e.dma_start(out=pv_t[:, s_, :], in_=pv_d[nl]), pvb[npv % 2])
                    ylru, ygla, ydn = ybr(Y_LRU), ybr(Y_GLA), ybr(Y_DN)
                    PHASES["lru"](l, pslot, ylru)
                    P.barrier()
                    if stop_after == "lru":
                        wstate["next"] += NU - 17
                        wstate["issued"] = max(wstate["issued"], wstate["next"])
                    elif stop_after == "gla":
                        PHASES["gla"](l, pslot, ygla, ydn)
                        P.barrier()
                        wstate["next"] += NU - 17 - 29
                        wstate["issued"] = max(wstate["issued"], wstate["next"])
                    elif stop_after == "gdn":
                        PHASES["gla"](l, pslot, ygla)
                        P.barrier()
                        PHASES["gdn"](l, pslot, ydn)
                        P.barrier()
                        wstate["next"] += NU - 17 - 29 - 32
                        wstate["issued"] = max(wstate["issued"], wstate["next"])
                    else:
                        PHASES["gla"](l, pslot, ygla)
                        P.barrier()
                        PHASES["gdn"](l, pslot, ydn)
                        P.barrier()
                        PHASES["merge"](l, pslot, ylru, ygla, ydn)
                        P.barrier()
                        PHASES["mlp"](l, pslot)
                        P.barrier()
                    if dbg and ti == ntiles - 1 and l == 0:
                        for wi, yb_ in enumerate((ylru, ygla, ydn)):
                            for c in range(8):
                                P.dma_out("sp", lambda e, wi=wi, c=c, yb_=yb_: e.dma_start(out=dbg_d[wi, :, c, :], in_=yb_[c].ap), yb_[c].b, osem)
                for k in range(KC):
                    P.dma_out("sp", lambda e, ti=ti, k=k: e.dma_start(out=oT_d[ti, :, k, :], in_=xr[k].ap), xr[k].b, osem)
            for i_, (v_, n_) in enumerate(st_views[:3]):
                P.dma_out("sp", lambda e, v_=v_, o_=st_off[i_], n_=n_: e.dma_start(out=sto_d[:, o_:o_ + n_], in_=v_), bstate, osem)
            for l_ in range(L):
                for h_ in range(4):
                    o_ = st_off[3] + (l_ * 4 + h_) * 256
                    P.dma_out("sp", lambda e, l_=l_, h_=h_, o_=o_: e.dma_start(out=sto_d[:, o_:o_ + 256], in_=glaS_t[:, l_, h_, :]), bglaS[l_][h_], osem)
                for g_ in range(2):
                    o_ = st_off[4] + (l_ * 8 + g_ * 4) * 128
                    P.dma_out("sp", lambda e, l_=l_, g_=g_, o_=o_: e.dma_start(out=sto_d[:, o_:o_ + 512], in_=dnS_t[:, l_, g_ * 4:(g_ + 1) * 4, :].rearrange("p h v -> p (h v)")), bdnS[l_][g_], osem)
            P.wait_all("sp", osem)

        PHASES["lru"] = phase_lru
        for _k in ("gla", "gdn", "merge", "mlp"):
            if ("phase_" + _k) in locals():
                PHASES[_k] = locals()["phase_" + _k]
        main_loop(stop_after)
        P.emit()
    return nc


def _to_fm(x, ntiles):
    return np.ascontiguousarray(x.reshape(ntiles, T, KC, 128).transpose(0, 3, 2, 1))


def _from_fm(o):
    nt = o.shape[0]
    return o.transpose(0, 3, 2, 1).reshape(nt * T, D)


def kernel(x, w_in, lru_conv_w, lru_conv_b, lru_wa, lru_ba, lru_wi, lru_bi, lru_lambda,
           gla_wa2, gla_ba2, gla_norm_g, dn_conv_w, dn_a_log, dn_dt_bias, dn_norm_g,
           w_branch, b_gate, w_out, ln1_g, ln1_b, mlp_w1, mlp_b1, mlp_w2, mlp_b2, ln2_g, ln2_b):
    f = lambda a: np.asarray(a, dtype=np.float32)
    x = f(x)
    B, S, _ = x.shape
    L = w_in.shape[0]
    ntiles = S // T
    Ws = np.stack([pack_layer_units(f(w_in[l]), f(lru_wa[l]), f(lru_wi[l]), f(w_branch[l]), f(w_out[l]), f(mlp_w1[l]), f(mlp_w2[l])) for l in range(L)])
    pvs = np.stack([pack_pv(f(lru_conv_w[l]), f(lru_conv_b[l]), f(lru_ba[l]), f(lru_bi[l]), f(lru_lambda[l]), f(gla_wa2[l]), f(gla_ba2[l]), f(gla_norm_g[l]),
                            f(dn_conv_w[l]), f(dn_a_log[l]), f(dn_dt_bias[l]), f(dn_norm_g[l]), f(b_gate[l]), f(ln1_g[l]), f(ln1_b[l]), f(mlp_b1[l]), f(mlp_b2[l]),
                            f(ln2_g[l]), f(ln2_b[l])) for l in range(L)])
    NT_L = 4
    nlaunch = ntiles // NT_L
    nc = build_program(NT_L, L)
    n = 8
    NS = L * (8 + 24 + 72 + 1024 + 1024)
    states = [np.zeros((128, NS), np.float32) for _ in range(n)]
    xfm = [_to_fm(x[c % B], ntiles) for c in range(n)]
    outs = [[] for _ in range(n)]
    for li in range(nlaunch):
        in_maps = [{"xT": np.ascontiguousarray(xfm[c][li * NT_L:(li + 1) * NT_L]), "W": Ws, "pv": pvs, "st_in": states[c]} for c in range(n)]
        res = run_bass_kernel_spmd(nc, in_maps, core_ids=list(range(n)))
        for c in range(n):
            outs[c].append(np.asarray(res.results[c]["oT"]))
            states[c] = np.ascontiguousarray(np.asarray(res.results[c]["st_out"]))
    out = np.stack([_from_fm(np.concatenate(outs[c], axis=0)) for c in range(B)])
    return out.astype(np.float32)
```

```python
import numpy as np
import concourse.bass as bass
import concourse.mybir as mybir
from concourse.bass_utils import run_bass_kernel_spmd

F32 = mybir.dt.float32
BF16 = mybir.dt.bfloat16
AF = mybir.ActivationFunctionType
ALU = mybir.AluOpType
AX = mybir.AxisListType

ENGS = ("pe", "act", "dve", "pool", "sp")


class Buf:
    __slots__ = ("name", "w", "r", "dsem", "dexp", "dreads", "dsems", "depoch")

    def __init__(self, name, dsem=None):
        self.name = name
        self.w = None
        self.r = {}
        self.dsem = dsem
        self.dexp = 0
        self.dreads = []
        self.dsems = None
        self.depoch = 0


class Prog:
    def __init__(self, nc, stack):
        self.nc = nc
        self.stack = stack
        self.q = {e: [] for e in ENGS}
        self.NEP = 4
        self.epoch = 0
        self.semep = {e: [stack.enter_context(nc.semaphore("s_%s%d" % (e, k))) for k in range(self.NEP)] for e in ENGS if e != "sp"}
        self.cnt = {e: [0] * self.NEP for e in ENGS}
        self.seen = {e: {} for e in ENGS}
        self.nsem = 0
        self.n_wait = 0

    def new_sem(self, name):
        self.nsem += 1
        name = "%s_%d" % (name, self.nsem)
        return self.stack.enter_context(self.nc.semaphore(name))

    def buf(self, name, dma=False):
        return Buf(name, self.new_sem("d_" + name) if dma else None)

    def _need(self, eng, waits, sem, val, key):
        if val <= 0:
            return
        if self.seen[eng].get(key, 0) >= val:
            return
        self.seen[eng][key] = val
        waits.append((sem, val))

    def _deps(self, eng, reads, writes):
        waits = []

        def need_op(we, ep, wi):
            if we == "pe" and eng == "pe":
                return
            self._need(eng, waits, self.semep[we][ep], wi, (we, ep))

        for b in reads:
            if b.w is not None:
                need_op(*b.w)
            if b.dsem is not None and b.dexp > 0:
                self._need(eng, waits, b.dsem, b.dexp, id(b.dsem))
        for b in writes:
            if b.w is not None:
                need_op(*b.w)
            for re_, (ep, ri) in b.r.items():
                need_op(re_, ep, ri)
            if b.dsem is not None and b.dexp > 0:
                self._need(eng, waits, b.dsem, b.dexp, id(b.dsem))
            for (s, v) in b.dreads:
                self._need(eng, waits, s, v, id(s))
        return waits

    def op(self, eng, fn, reads=(), writes=()):
        waits = self._deps(eng, reads, writes)
        ep = self.epoch
        self.cnt[eng][ep] += 1
        idx = self.cnt[eng][ep]
        self.q[eng].append((fn, waits, (self.semep[eng][ep], 1)))
        for b in reads:
            b.r[eng] = (ep, idx)
        for b in writes:
            b.w = (eng, ep, idx)
            b.r = {}
            b.dreads = []
        return idx

    def dma_in(self, qeng, fn, dst, reads=()):
        assert dst.dsem is not None
        waits = self._deps(qeng, reads, [dst])
        if dst.dsems is not None and dst.depoch != self.epoch:
            dst.depoch = self.epoch
            dst.dsem = dst.dsems[self.epoch]
            dst.dexp = 0
        dst.dexp += 16
        self.q[qeng].append((fn, waits, (dst.dsem, 16)))
        dst.w = None
        dst.r = {}
        dst.dreads = []

    def dma_out(self, qeng, fn, src, osem_state):
        waits = self._deps(qeng, [src], [])
        osem_state[1] += 16
        self.q[qeng].append((fn, waits, (osem_state[0], 16)))
        src.dreads.append((osem_state[0], osem_state[1]))

    def wait_all(self, eng, sem_state):
        self.q[eng].append((None, [(sem_state[0], sem_state[1])], None))

    def emit(self):
        nc = self.nc
        engobj = {"pe": "tensor", "act": "scalar", "dve": "vector", "pool": "gpsimd", "sp": "sync"}
        with nc.Block() as block:
            for e in ENGS:
                items = self.q[e]
                if not items:
                    continue

                def body(eng, items=items, e=e):
                    embed = False
                    for (fn, waits, inc) in items:
                        emb = None
                        if embed and fn is not None and waits:
                            emb = waits[-1]
                            waits = waits[:-1]
                        for (s, v) in waits:
                            eng.wait_ge(s, v)
                            self.n_wait += 1
                        if fn is None:
                            continue
                        ins = fn(eng)
                        if emb is not None:
                            ins._wait_ge(emb[0], emb[1])
                        if inc is not None:
                            ins.then_inc(inc[0], inc[1])

                getattr(block, engobj[e])(body)

    def barrier(self):
        for e in ("pe", "act", "dve"):
            waits = []
            for e2 in ("pe", "act", "dve"):
                if e2 != e:
                    for ep in range(self.NEP):
                        self._need(e, waits, self.semep[e2][ep], self.cnt[e2][ep], (e2, ep))
            if waits:
                self.q[e].append((None, waits, None))


D = 2048
KC = 16
T = 512
DIN = 15392
ALPHA = (2.0 * 4) ** 0.25
LN_EPS = 1e-5
NORM_EPS = 1e-6
C_LRUX, C_LRUY, C_Q, C_K, C_V, C_ALR, C_R = 0, 1024, 2048, 2560, 3072, 4096, 4112
C_DQKV, C_DB, C_DA, C_DZ, C_GATE = 5136, 8208, 8216, 8224, 9248
NSLOT = 6

PV = {}
_o = 0
for _n, _w in [("lcw", 32), ("lcb", 8), ("lba", 8), ("lbi", 8), ("llam", 8), ("glag", 2), ("dcw", 96), ("dng", 1),
               ("bg", 48), ("ln1g", 16), ("ln1b", 16), ("ln2g", 16), ("ln2b", 16), ("b2", 16), ("b1", 64),
               ("alog", 8), ("dtb", 8), ("ba2", 512), ("wa2", 512)]:
    PV[_n] = (_o, _w)
    _o += _w
NPV = _o


def unit_list():
    u = [("lrug",)]
    for c in range(8):
        u += [("fm", C_LRUX + c * 128), ("fm", C_LRUY + c * 128)]
    u += [("misc",)]
    for j in range(4):
        u += [("tm", C_K, j)]
    for g in range(2):
        for j in range(4):
            u += [("tm", C_V + g * 512, j)]
    for h in range(4):
        u += [("fm", C_Q + h * 128), ("fm", C_K + h * 128), ("fm", C_R + h * 256), ("fm", C_R + h * 256 + 128)]
    for h in range(8):
        u += [("fm", C_DQKV + h * 128), ("fm", C_DQKV + 1024 + h * 128), ("fm", C_DQKV + 2048 + h * 128), ("fm", C_DZ + h * 128)]
    for m in range(16):
        for b in range(3):
            u += [("fm", C_GATE + b * 2048 + m * 128)]
        u += [("brA", m), ("brB", m)]
    for m in range(16):
        u += [("out", m)]
    for f in range(64):
        u += [("w1", f)]
    for m in range(16):
        for j in range(4):
            u += [("w2", m, j)]
    return u


UNITS = unit_list()
NU = len(UNITS)


def pack_layer_units(w_in, lru_wa, lru_wi, w_branch, w_out, w1, w2):
    out = np.zeros((NU, 128, 2048), np.float32)

    def fm(W, c0, kc):
        return W[:, c0:c0 + 128].reshape(kc, 128, 128).transpose(1, 0, 2).reshape(128, kc * 128)

    for i, tag in enumerate(UNITS):
        k = tag[0]
        if k == "fm":
            out[i] = fm(w_in, tag[1], 16)
        elif k == "tm":
            c0, j = tag[1], tag[2]
            blk = w_in[:, c0:c0 + 512].reshape(16, 128, 512)[4 * j:4 * j + 4]
            out[i] = blk.transpose(1, 0, 2).reshape(128, 2048)
        elif k == "misc":
            blk = np.concatenate([w_in[:, C_ALR:C_ALR + 16], w_in[:, C_DB:C_DB + 16]], axis=1)
            out[i, :, :512] = blk.reshape(16, 128, 32).transpose(1, 0, 2).reshape(128, 512)
        elif k == "lrug":
            bd = np.zeros((128, 8, 2, 128), np.float32)
            for c in range(8):
                for gi, wg in enumerate((lru_wa, lru_wi)):
                    bd[0:64, c, gi, 0:64] = wg[2 * c]
                    bd[64:128, c, gi, 64:128] = wg[2 * c + 1]
            out[i] = bd.reshape(128, 2048)
        elif k == "brA":
            m = tag[1]
            out[i, :, 0:1024] = fm(w_branch[0], m * 128, 8)
            out[i, :, 1024:2048] = fm(w_branch[1], m * 128, 8)
        elif k == "brB":
            m = tag[1]
            out[i, :, 0:1024] = fm(w_branch[2], m * 128, 8)
        elif k == "out":
            out[i] = fm(w_out, tag[1] * 128, 16)
        elif k == "w1":
            out[i] = fm(w1, tag[1] * 128, 16)
        elif k == "w2":
            m, j = tag[1], tag[2]
            blk = w2[:, m * 128:(m + 1) * 128].reshape(64, 128, 128)[16 * j:16 * j + 16]
            out[i] = blk.transpose(1, 0, 2).reshape(128, 2048)
    return out


def pack_pv(lcw, lcb, lba, lbi, llam, wa2, ba2, glag, dcw, alog, dtb, dng, bgate, ln1g, ln1b, b1, b2, ln2g, ln2b):
    pv = np.zeros((128, NPV), np.float32)

    def put(name, arr):
        o, w = PV[name]
        pv[:, o:o + w] = arr

    def fmv(v, nch):
        return v.reshape(nch, 128).T

    put("lcw", lcw.reshape(4, 8, 128).transpose(2, 1, 0).reshape(128, 32))
    put("lcb", fmv(lcb, 8)); put("lba", fmv(lba, 8)); put("lbi", fmv(lbi, 8)); put("llam", fmv(llam, 8))
    put("glag", fmv(glag, 2))
    put("dcw", dcw.reshape(4, 24, 128).transpose(2, 1, 0).reshape(128, 96))
    put("dng", dng.reshape(128, 1))
    put("bg", bgate.reshape(3, 16, 128).transpose(2, 0, 1).reshape(128, 48))
    put("ln1g", fmv(ln1g, 16)); put("ln1b", fmv(ln1b, 16)); put("ln2g", fmv(ln2g, 16)); put("ln2b", fmv(ln2b, 16))
    put("b2", fmv(b2, 16)); put("b1", fmv(b1, 64))
    put("alog", np.broadcast_to(alog[None, :], (128, 8)))
    put("dtb", np.broadcast_to(dtb[None, :], (128, 8)))
    put("ba2", np.broadcast_to(ba2[None, :], (128, 512)))
    o, w = PV["wa2"]
    pv[0:16, o:o + w] = wa2
    return pv


class Tl:
    __slots__ = ("ap", "b")

    def __init__(self, ap, b):
        self.ap = ap
        self.b = b


def build_program(ntiles, L, dbg=False, stop_after=None):
    from contextlib import ExitStack
    nc = bass.Bass("TRN2", target_bir_lowering=False)
    xT_d = nc.dram_tensor("xT", [ntiles, 128, KC, T], F32, kind="ExternalInput").ap()
    W_d = nc.dram_tensor("W", [L, NU, 128, 2048], F32, kind="ExternalInput").ap()
    pv_d = nc.dram_tensor("pv", [L, 128, NPV], F32, kind="ExternalInput").ap()
    oT_d = nc.dram_tensor("oT", [ntiles, 128, KC, T], F32, kind="ExternalOutput").ap()
    NS = L * (8 + 24 + 72 + 1024 + 1024)
    sti_d = nc.dram_tensor("st_in", [128, NS], F32, kind="ExternalInput").ap()
    sto_d = nc.dram_tensor("st_out", [128, NS], F32, kind="ExternalOutput").ap()
    if dbg:
        dbg_d = nc.dram_tensor("dbg", [3, 128, 8, T], BF16, kind="ExternalOutput").ap()
    st = ExitStack()
    with st:
        P = Prog(nc, st)

        def sbt(name, shape, dt=F32):
            return st.enter_context(nc.sbuf_tensor(name, shape, dt))

        xr_t = sbt("xr", [128, KC, T]); xb_t = sbt("xb", [128, KC, T], BF16)
        xr = [Tl(xr_t[:, k, :], P.buf("xr%d" % k, dma=True)) for k in range(KC)]
        xb = [Tl(xb_t[:, k, :], P.buf("xb%d" % k)) for k in range(KC)]
        ring_t = sbt("ring", [128, NSLOT, 2048], BF16)
        ring = [Tl(ring_t[:, s, :], P.buf("ring%d" % s, dma=True)) for s in range(NSLOT)]
        for r_ in ring:
            r_.b.dsems = [r_.b.dsem] + [P.new_sem("ringe") for _ in range(P.NEP - 1)]
        pv_t = sbt("pvt", [128, 2, NPV])
        pvb = [P.buf("pv%d" % i, dma=True) for i in range(2)]
        lrug_t = sbt("lrug", [128, 2048], BF16); lrug = Tl(lrug_t[:], P.buf("lrug"))
        misc_t = sbt("miscp", [128, 512], BF16); miscp = Tl(misc_t[:], P.buf("miscp"))
        lru_h_t = sbt("lru_h", [128, L, 8]); lru_tail_t = sbt("lru_tail", [128, L, 8, 3])
        dn_tail_t = sbt("dn_tail", [128, L, 24, 3])
        glaS_t = sbt("glaS", [128, L, 4, 256]); dnS_t = sbt("dnS", [128, L, 8, 128])
        glaSb_t = sbt("glaSb", [128, 4, 256], BF16); dnSb_t = sbt("dnSb", [128, 8, 128], BF16)
        bstate = P.buf("state")
        bglaS = [[P.buf("glaS") for h in range(4)] for l in range(L)]
        bglaSb = [P.buf("glaSb") for h in range(4)]
        bdnS = [[P.buf("dnS") for g in range(2)] for l in range(L)]
        bdnSb = [P.buf("dnSb") for g in range(2)]
        ident_t = sbt("ident", [128, 128]); ones_f_t = sbt("ones_f", [128, 128]); nones_f_t = sbt("nones_f", [128, 128])
        ones_b_t = sbt("ones_b", [128, 128], BF16)
        LS_t = sbt("LS", [128, 128]); UI_t = sbt("UI", [128, 128])
        LSg_t = sbt("LSg", [128, 128]); UIg_t = sbt("UIg", [128, 128])
        bconst = P.buf("const")
        arena_t = sbt("arena", [128, 19968])
        NA = 19968
        ps = []
        for i in range(8):
            pt = st.enter_context(nc.psum_tensor("ps%d" % i, [128, 512], F32))
            ps.append(Tl(pt[:], P.buf("ps%d" % i)))
        osem = [P.new_sem("osem"), 0]

        def bl(x):
            return [t.b if isinstance(t, Tl) else t for t in x]

        def act(out, in_, func, R, W, bias=0.0, scale=1.0):
            P.op("act", lambda e: e.activation(out=out, in_=in_, func=func, bias=bias, scale=scale), bl(R), bl(W))

        def tt(out, a, b, op, R, W):
            P.op("dve", lambda e: e.tensor_tensor(out=out, in0=a, in1=b, op=op), bl(R), bl(W))

        def ts(out, a, s1, op0, R, W, s2=None, op1=None):
            if op1 is None:
                P.op("dve", lambda e: e.tensor_scalar(out=out, in0=a, scalar1=s1, scalar2=None, op0=op0), bl(R), bl(W))
            else:
                P.op("dve", lambda e: e.tensor_scalar(out=out, in0=a, scalar1=s1, scalar2=s2, op0=op0, op1=op1), bl(R), bl(W))

        def stt(out, a, s, b, op0, op1, R, W):
            P.op("dve", lambda e: e.scalar_tensor_tensor(out=out, in0=a, scalar=s, in1=b, op0=op0, op1=op1), bl(R), bl(W))

        def cp(out, in_, R, W, eng="dve"):
            if eng == "act":
                act(out, in_, AF.Copy, R, W)
            else:
                P.op("dve", lambda e: e.tensor_copy(out=out, in_=in_), bl(R), bl(W))

        def mms(lst, R, W):
            def fn(e):
                ins = None
                for (o, l_, r_, s0, s1) in lst:
                    ins = e.matmul(o, lhsT=l_, rhs=r_, start=s0, stop=s1)
                return ins
            P.op("pe", fn, bl(R), bl(W))

        def transp(out, in_, R, W):
            P.op("pe", lambda e: e.transpose(out, in_, ident_t[:]), bl(R) + [bconst], bl(W))

        class Arena:
            def __init__(self, base, limit=NA):
                self.o = base; self.limit = limit
            def f32(self, n, name="t"):
                a = arena_t[:, self.o:self.o + n]; self.o += n
                assert self.o <= self.limit, "arena overflow"
                return Tl(a, P.buf(name))
            def bf16(self, n, name="t"):
                n2 = (n + 1) // 2
                a = arena_t[:, self.o:self.o + n2].bitcast(BF16); self.o += n2
                assert self.o <= self.limit, "arena overflow"
                return Tl(a, P.buf(name))

        stream = [(l, u) for _t in range(ntiles) for l in range(L) for u in range(NU)]
        wstate = {"issued": 0, "next": 0}

        def w_issue():
            i = wstate["issued"]
            if i >= len(stream):
                return
            l, u = stream[i]
            slot = ring[i % NSLOT]
            P.dma_in("pool", lambda e, l=l, u=u, slot=slot: e.dma_start(out=slot.ap, in_=W_d[l, u]), slot.b)
            wstate["issued"] += 1

        def wget(l, tag):
            i = wstate["next"]
            assert stream[i][0] == l and UNITS[stream[i][1]] == tag, (stream[i], UNITS[stream[i][1]], tag)
            while wstate["issued"] < min(len(stream), i + NSLOT - 3):
                w_issue()
            wstate["next"] += 1
            return ring[i % NSLOT]

        def proj16(pst, u, M=128, col0=0, stride=128):
            lst = [(pst.ap[0:M, :], u.ap[:, kc * stride + col0: kc * stride + col0 + M], xb[kc].ap, kc == 0, kc == KC - 1)
                   for kc in range(KC)]
            mms(lst, [u] + xb, [pst])

        def pool_op(fn, R, W):
            P.op("pool", fn, bl(R), bl(W))

        pool_op(lambda e: e.memset(ident_t[:], 0.0), [], [bconst])
        pool_op(lambda e: e.affine_select(out=ident_t[:], in_=ident_t[:], pattern=[[-1, 128]], compare_op=ALU.not_equal, fill=1.0, base=0, channel_multiplier=1), [bconst], [bconst])
        pool_op(lambda e: e.memset(ones_f_t[:], 1.0), [], [bconst])
        pool_op(lambda e: e.memset(nones_f_t[:], -1.0), [], [bconst])
        pool_op(lambda e: e.memset(ones_b_t[:], 1.0), [], [bconst])
        for (mt, cmpop, lower) in [(LS_t, ALU.is_gt, True), (UI_t, ALU.is_ge, False)]:
            pool_op(lambda e, mt=mt: e.memset(mt[:], 0.0), [], [bconst])
            for b_ in range(2):
                blk = mt[64 * b_:64 * b_ + 64, 64 * b_:64 * b_ + 64]
                pool_op(lambda e, blk=blk: e.memset(blk, 1.0), [bconst], [bconst])
                pat = [[-1, 64]] if lower else [[1, 64]]
                cm = 1 if lower else -1
                pool_op(lambda e, blk=blk, pat=pat, cm=cm, cmpop=cmpop: e.affine_select(out=blk, in_=blk, pattern=pat, compare_op=cmpop, fill=0.0, base=0, channel_multiplier=cm), [bconst], [bconst])
        pool_op(lambda e: e.tensor_scalar(out=LSg_t[:], in0=LS_t[:], scalar1=-1.0 / 16.0, scalar2=None, op0=ALU.mult), [bconst], [bconst])
        pool_op(lambda e: e.tensor_scalar(out=UIg_t[:], in0=UI_t[:], scalar1=-1.0 / 16.0, scalar2=None, op0=ALU.mult), [bconst], [bconst])
        bload = P.buf("stload", dma=True)
        st_views = [(lru_h_t[:].rearrange("p l c -> p (l c)"), L * 8), (lru_tail_t[:].rearrange("p l c k -> p (l c k)"), L * 24),
                    (dn_tail_t[:].rearrange("p l c k -> p (l c k)"), L * 72), (glaS_t[:].rearrange("p l h v -> p (l h v)"), L * 1024),
                    (dnS_t[:].rearrange("p l h v -> p (l h v)"), L * 1024)]
        _o = 0
        st_off = []
        for (v_, n_) in st_views:
            st_off.append(_o)
            P.dma_in("sp", lambda e, v_=v_, o_=_o, n_=n_: e.dma_start(out=v_, in_=sti_d[:, o_:o_ + n_]), bload)
            _o += n_
        P.op("dve", lambda e: e.tensor_copy(out=lru_h_t[:, 0, 0:1], in_=lru_h_t[:, 0, 0:1]), [bload], [bstate])
        pool_op(lambda e: e.memset(glaSb_t[:], 0.0), [], bglaSb)
        pool_op(lambda e: e.memset(dnSb_t[:], 0.0), [], bdnSb)
        for l in range(L):
            for h in range(4):
                bglaS[l][h].w = bstate.w
            for g in range(2):
                bdnS[l][g].w = bstate.w

        def pvc(slot, name, j=0, n=1):
            o, w = PV[name]
            return pv_t[:, slot, o + j:o + j + n]

        def layernorm(A, pslot, gname, bname, t1s=None):
            s1, s2 = ps[6], ps[7]
            lnset = [(A.bf16(T, "lnyb"), A.bf16(T, "lnyq"), (t1s[_i] if t1s is not None else A.f32(T, "lnt"))) for _i in range(2)]
            for m in range(KC):
                yb_, yq_, _ = lnset[m % 2]
                act(yb_.ap, xr[m].ap, AF.Copy, [xr[m]], [yb_])
                act(yq_.ap, xr[m].ap, AF.Square, [xr[m]], [yq_])
                mms([(s1.ap, ones_b_t[:], yb_.ap, m == 0, m == KC - 1)], [yb_, bconst], [s1])
                mms([(s2.ap, ones_b_t[:], yq_.ap, m == 0, m == KC - 1)], [yq_, bconst], [s2])
            msq = A.f32(T, "msq"); var = msq; rstd = A.f32(T, "rstd"); nmr = A.f32(T, "nmr")
            act(msq.ap, s1.ap, AF.Square, [s1], [msq], scale=1.0 / D)
            stt(var.ap, s2.ap, 1.0 / D, msq.ap, ALU.mult, ALU.subtract, [s2, msq], [msq])
            ts(var.ap, var.ap, LN_EPS, ALU.add, [var], [var])
            act(var.ap, var.ap, AF.Sqrt, [var], [var])
            P.op("dve", lambda e: e.reciprocal(out=rstd.ap, in_=var.ap), bl([var]), bl([rstd]))
            stt(nmr.ap, s1.ap, -1.0 / D, rstd.ap, ALU.mult, ALU.mult, [s1, rstd], [nmr])
            for m in range(KC):
                t1 = lnset[m % 2][2]
                tt(t1.ap, xr[m].ap, rstd.ap, ALU.mult, [xr[m], rstd], [t1])
                tt(t1.ap, t1.ap, nmr.ap, ALU.add, [t1, nmr], [t1])
                act(xr[m].ap, t1.ap, AF.Identity, [t1, pvb[pslot]], [xr[m]], bias=pvc(pslot, bname, m), scale=pvc(pslot, gname, m))
                cp(xb[m].ap, xr[m].ap, [xr[m]], [xb[m]])

        Y_LRU, Y_GLA, Y_DN = 0, 2048, 4096
        TMP0 = 6144

        def ybr(which):
            return [Tl(arena_t[:, which + c * 256: which + (c + 1) * 256].bitcast(BF16), P.buf("ybr")) for c in range(8)]

        def phase_lru(l, pslot, ylru):
            A = Arena(TMP0)
            u0 = wget(l, ("lrug",))
            cp(lrug.ap, u0.ap, [u0], [lrug])
            pb_ = pvb[pslot]
            cl = A.f32(8, "cl"); cl2 = A.f32(8, "cl2")
            act(cl.ap, pvc(pslot, "llam", 0, 8), AF.Exp, [pb_], [cl], scale=-1.0)
            act(cl.ap, cl.ap, AF.Ln, [cl], [cl], bias=1.0)
            ts(cl2.ap, cl.ap, -16.0, ALU.mult, [cl], [cl2])
            ts(cl.ap, cl.ap, -8.0, ALU.mult, [cl], [cl])
            sets = []
            for _ in range(2):
                sets.append(dict(xpad=A.f32(T + 3, "xpad"), xc=A.f32(T, "xc"), xcb=A.bf16(T, "xcb"), r=A.f32(T, "r"), ig=A.f32(T, "i"),
                                 a=A.f32(T, "a"), a2=A.f32(T, "a2"), hh=A.f32(T, "h"), y2=A.f32(T, "y2"), sg=A.f32(T, "sg")))
            for c in range(8):
                S_ = sets[c % 2]
                ua = wget(l, ("fm", C_LRUX + c * 128))
                pa = ps[c % 2]
                proj16(pa, ua)
                xpad = S_["xpad"]; xc = S_["xc"]; xcb = S_["xcb"]
                cp(xpad.ap[:, 0:3], lru_tail_t[:, l, c, :], [bstate], [xpad])
                act(xpad.ap[:, 3:T + 3], pa.ap, AF.Copy, [pa], [xpad])
                cp(lru_tail_t[:, l, c, :], xpad.ap[:, T:T + 3], [xpad], [bstate])
                act(xc.ap, xpad.ap[:, 3:T + 3], AF.Identity, [xpad, pb_], [xc], bias=pvc(pslot, "lcb", c), scale=pvc(pslot, "lcw", c * 4 + 3))
                for k in (2, 1, 0):
                    stt(xc.ap, xpad.ap[:, k:k + T], pvc(pslot, "lcw", c * 4 + k), xc.ap, ALU.mult, ALU.add, [xpad, xc, pb_], [xc])
                act(xcb.ap, xc.ap, AF.Copy, [xc], [xcb])
                pr, pi = ps[2], ps[3]
                mms([(pr.ap, lrug.ap[:, (c * 2 + 0) * 128:(c * 2 + 1) * 128], xcb.ap, True, True)], [lrug, xcb], [pr])
                mms([(pi.ap, lrug.ap[:, (c * 2 + 1) * 128:(c * 2 + 2) * 128], xcb.ap, True, True)], [lrug, xcb], [pi])
                r = S_["r"]; ig = S_["ig"]; a = S_["a"]; a2 = S_["a2"]
                act(r.ap, pr.ap, AF.Sigmoid, [pr, pb_], [r], bias=pvc(pslot, "lba", c))
                act(ig.ap, pi.ap, AF.Sigmoid, [pi, pb_], [ig], bias=pvc(pslot, "lbi", c))
                act(a.ap, r.ap, AF.Exp, [r, cl], [a], scale=cl.ap[:, c:c + 1])
                act(a2.ap, r.ap, AF.Exp, [r, cl2], [a2], scale=cl2.ap[:, c:c + 1])
                act(a2.ap, a2.ap, AF.Sqrt, [a2], [a2], bias=1.0, scale=-1.0)
                tt(ig.ap, ig.ap, xc.ap, ALU.mult, [ig, xc], [ig])
                tt(ig.ap, ig.ap, a2.ap, ALU.mult, [ig, a2], [ig])
                hh = S_["hh"]
                P.op("dve", lambda e, hh=hh, a=a, ig=ig, c=c: e.tensor_tensor_scan(out=hh.ap, data0=a.ap, data1=ig.ap, initial=lru_h_t[:, l, c:c + 1], op0=ALU.mult, op1=ALU.add), bl([a, ig, bstate]), bl([hh]))
                cp(lru_h_t[:, l, c:c + 1], hh.ap[:, T - 1:T], [hh], [bstate])
                ub = wget(l, ("fm", C_LRUY + c * 128))
                pyb = ps[4 + c % 2]
                proj16(pyb, ub)
                y2 = S_["y2"]; sg = S_["sg"]
                act(y2.ap, pyb.ap, AF.Square, [pyb], [y2])
                ts(y2.ap, y2.ap, 0.044715, ALU.mult, [y2], [y2], s2=1.0, op1=ALU.add)
                tt(y2.ap, y2.ap, pyb.ap, ALU.mult, [y2, pyb], [y2])
                act(sg.ap, y2.ap, AF.Sigmoid, [y2], [sg], scale=1.5957691216057308)
                tt(sg.ap, sg.ap, pyb.ap, ALU.mult, [sg, pyb], [sg])
                tt(ylru[c].ap, sg.ap, hh.ap, ALU.mult, [sg, hh], [ylru[c]])


        def phase_gla(l, pslot, ygla, dbgt=None):
            A = Arena(TMP0)
            pb_ = pvb[pslot]
            alrT = A.f32(T, "alrT")
            Gtok = [A.f32(512, "Gtok") for _ in range(4)]
            kd = [A.bf16(512, "kd") for _ in range(4)]
            vtok = [A.bf16(1024, "vtok") for _ in range(4)]
            E1 = A.f32(T, "E1"); E2 = A.f32(T, "E2"); dec = A.f32(8, "dec"); qe = A.bf16(T, "qe"); ke = A.bf16(T, "ke")
            attm = [A.bf16(128, "attm") for _ in range(2)]
            sq = [A.f32(T, "sq") for _ in range(2)]; rstd = A.f32(T, "rstd"); sr = A.f32(T, "sr"); t1 = A.f32(T, "t1")
            Dx = A.f32(512, "Dx"); zt = A.f32(512, "zt")
            um = wget(l, ("misc",))
            cp(miscp.ap, um.ap[:, 0:512], [um], [miscp])
            pst = ps[0]
            mms([(pst.ap[0:16, :], miscp.ap[:, kc * 32: kc * 32 + 16], xb[kc].ap, kc == 0, kc == KC - 1) for kc in range(KC)], [miscp] + xb, [pst])
            act(alrT.ap[0:16, :], pst.ap[0:16, :], AF.Copy, [pst], [alrT])
            uk = [wget(l, ("tm", C_K, j)) for j in range(4)]
            wa2 = pvc(pslot, "wa2", 0, 512)
            for tb in range(4):
                pz = ps[2]
                mms([(pz.ap, alrT.ap[0:16, tb * 128:(tb + 1) * 128], wa2[0:16, :], True, True)], [alrT, pb_], [pz])
                tt(zt.ap, pz.ap, pvc(pslot, "ba2", 0, 512), ALU.add, [pz, pb_], [zt])
                act(zt.ap, zt.ap, AF.Exp, [zt], [zt], scale=-1.0)
                act(Gtok[tb].ap, zt.ap, AF.Ln, [zt], [Gtok[tb]], bias=1.0)
                pk = ps[tb % 2]
                mms([(pk.ap, xb[kc].ap[:, tb * 128:(tb + 1) * 128], uk[kc // 4].ap[:, (kc % 4) * 512:(kc % 4 + 1) * 512], kc == 0, kc == KC - 1) for kc in range(KC)], uk + xb, [pk])
                psD = ps[3]
                mms([(psD.ap, LSg_t[:], Gtok[tb].ap, True, True)], [Gtok[tb], bconst], [psD])
                act(Dx.ap, psD.ap, AF.Exp, [psD], [Dx])
                tt(kd[tb].ap, pk.ap, Dx.ap, ALU.mult, [pk, Dx], [kd[tb]])
            for g in range(2):
                uv = [wget(l, ("tm", C_V + g * 512, j)) for j in range(4)]
                for tb in range(4):
                    pv_ = ps[4 + tb % 2]
                    mms([(pv_.ap, xb[kc].ap[:, tb * 128:(tb + 1) * 128], uv[kc // 4].ap[:, (kc % 4) * 512:(kc % 4 + 1) * 512], kc == 0, kc == KC - 1) for kc in range(KC)], uv + xb, [pv_])
                    act(vtok[tb].ap[:, g * 512:(g + 1) * 512], pv_.ap, AF.Copy, [pv_], [vtok[tb]])
            for h in range(4):
                Sv = glaS_t[:, l, h, :]
                bS = bglaS[l][h]
                act(glaSb_t[:, h, :], Sv, AF.Copy, [bS], [bglaSb[h]])
                pbt = ps[2]
                mms([(pbt.ap[:, tb * 128:(tb + 1) * 128], Gtok[tb].ap[:, h * 128:(h + 1) * 128], UIg_t[:], True, True) for tb in range(4)], Gtok + [bconst], [pbt])
                act(E1.ap, pbt.ap, AF.Exp, [pbt], [E1])
                act(E2.ap, pbt.ap, AF.Exp, [pbt], [E2], scale=-1.0)
                cp(dec.ap, E1.ap[:, 63:512:64], [E1], [dec])
                uq = wget(l, ("fm", C_Q + h * 128))
                pq = ps[0]
                proj16(pq, uq)
                stt(qe.ap, pq.ap, 128.0 ** -0.5, E1.ap, ALU.mult, ALU.mult, [pq, E1], [qe])
                uk_ = wget(l, ("fm", C_K + h * 128))
                pk = ps[1]
                proj16(pk, uk_)
                tt(ke.ap, pk.ap, E2.ap, ALU.mult, [pk, E2], [ke])
                po = [ps[4], ps[5]]
                for tb in range(4):
                    pa = ps[3]
                    tsl = slice(tb * 128, (tb + 1) * 128)
                    mms([(pa.ap[:, 0:128], ke.ap[:, tsl], qe.ap[:, tsl], True, True)], [ke, qe], [pa])
                    am = attm[tb % 2]
                    tt(am.ap, pa.ap[:, 0:128], UI_t[:], ALU.mult, [pa, bconst], [am])
                    mms([(po[vc].ap[:, tsl], vtok[tb].ap[:, h * 256 + vc * 128: h * 256 + (vc + 1) * 128], am.ap, True, False) for vc in range(2)], [vtok[tb], am], po)
                    for half in range(2):
                        n = 2 * tb + half
                        r0 = 64 * half
                        csl = slice(n * 64, (n + 1) * 64)
                        mms([(po[vc].ap[:, csl], glaSb_t[:, h, vc * 128:(vc + 1) * 128], qe.ap[:, csl], False, half == 1) for vc in range(2)], [bglaSb[h], qe], po)
                        pu = ps[6]
                        mms([(pu.ap[:, 0:256], kd[tb].ap[r0:r0 + 64, h * 128:(h + 1) * 128], vtok[tb].ap[r0:r0 + 64, h * 256:(h + 1) * 256], True, True)], [kd[tb], vtok[tb]], [pu])
                        stt(Sv, Sv, dec.ap[:, n:n + 1], pu.ap[:, 0:256], ALU.mult, ALU.add, [bS, dec, pu], [bS])
                        act(glaSb_t[:, h, :], Sv, AF.Copy, [bS], [bglaSb[h]])
                if dbgt is not None and h == 3:
                    cp(dbgt[0].ap, Gtok[0].ap, [Gtok[0]], [dbgt[0]])
                    cp(dbgt[1].ap, kd[0].ap, [kd[0]], [dbgt[1]])
                    cp(dbgt[2].ap, E1.ap, [E1], [dbgt[2]])
                    cp(dbgt[3].ap, qe.ap, [qe], [dbgt[3]])
                    cp(dbgt[4].ap, ke.ap, [ke], [dbgt[4]])
                    cp(dbgt[5].ap, vtok[0].ap[:, 0:512], [vtok[0]], [dbgt[5]])
                    cp(dbgt[6].ap, po[0].ap, [po[0]], [dbgt[6]])
                    cp(dbgt[7].ap, po[1].ap, [po[1]], [dbgt[7]])
                for vc in range(2):
                    act(sq[vc].ap, po[vc].ap, AF.Square, [po[vc]], [sq[vc]])
                pss = ps[7]
                mms([(pss.ap, ones_f_t[:], sq[0].ap, True, False), (pss.ap, ones_f_t[:], sq[1].ap, False, True)], [sq[0], sq[1], bconst], [pss])
                act(rstd.ap, pss.ap, AF.Sqrt, [pss], [rstd], bias=NORM_EPS, scale=1.0 / 256.0)
                P.op("dve", lambda e: e.reciprocal(out=rstd.ap, in_=rstd.ap), bl([rstd]), bl([rstd]))
                for vc in range(2):
                    ur = wget(l, ("fm", C_R + h * 256 + vc * 128))
                    pr = ps[vc]
                    proj16(pr, ur)
                    act(sr.ap, pr.ap, AF.Silu, [pr], [sr])
                    stt(t1.ap, po[vc].ap, pvc(pslot, "glag", vc), rstd.ap, ALU.mult, ALU.mult, [po[vc], pb_, rstd], [t1])
                    tt(ygla[h * 2 + vc].ap, t1.ap, sr.ap, ALU.mult, [t1, sr], [ygla[h * 2 + vc]])


        def phase_gdn(l, pslot, ydn):
            A = Arena(TMP0)
            pb_ = pvb[pslot]
            f = lambda n, nm="g": A.f32(n, nm)
            h16 = lambda n, nm="g": A.bf16(n, nm)
            pad = f(T + 3); cv = f(T); qn = f(T); kn = f(T); sqt = f(T); rn = f(T); Ebc = f(T)
            qnb = h16(T); knb = h16(T); qgT = h16(T); KbgT = h16(T)
            kg = [h16(128) for _ in range(4)]; Vb = [h16(128) for _ in range(4)]; TT = [h16(128) for _ in range(4)]; AqkT = [h16(128) for _ in range(4)]
            diag = f(128); GM = f(128); EA = f(128); EB = f(128); NA = f(128); NB = f(128); A2 = f(128); B2 = f(128); Rm = f(128)
            Wm = h16(128); vnew = h16(128); egl = f(8); rstd = f(T); sz = f(T); t1 = f(T)
            btok = f(32); gtok = f(32); gctok = f(32); ekg = f(32); bgc = f(32); negA = f(8)
            pt = ps[7]
            for tb in range(4):
                mms([(pt.ap[:, tb * 16:(tb + 1) * 16], xb[kc].ap[:, tb * 128:(tb + 1) * 128], miscp.ap[:, kc * 32 + 16: kc * 32 + 32], kc == 0, kc == KC - 1) for kc in range(KC)], [miscp] + xb, [pt])
            pt3 = pt.ap[:, 0:64].rearrange("p (t c) -> p t c", c=16)
            v3 = lambda tl: tl.ap.rearrange("p (t c) -> p t c", c=8)
            act(v3(btok), pt3[:, :, 0:8], AF.Sigmoid, [pt], [btok])
            for tb in range(4):
                tt(gtok.ap[:, tb * 8:(tb + 1) * 8], pt.ap[:, tb * 16 + 8: tb * 16 + 16], pvc(pslot, "dtb", 0, 8), ALU.add, [pt, pb_], [gtok])
            act(gtok.ap, gtok.ap, AF.Exp, [gtok], [gtok])
            act(gtok.ap, gtok.ap, AF.Ln, [gtok], [gtok], bias=1.0)
            act(negA.ap, pvc(pslot, "alog", 0, 8), AF.Exp, [pb_], [negA])
            for tb in range(4):
                stt(gtok.ap[:, tb * 8:(tb + 1) * 8], gtok.ap[:, tb * 8:(tb + 1) * 8], -1.0, negA.ap, ALU.mult, ALU.mult, [gtok, negA], [gtok])
            pc = ps[6]
            for tb in range(4):
                mms([(pc.ap[:, tb * 8:(tb + 1) * 8], UI_t[:], gtok.ap[:, tb * 8:(tb + 1) * 8], True, True)], [gtok, bconst], [pc])
                mms([(pc.ap[:, 32 + tb * 8: 32 + (tb + 1) * 8], LS_t[:], gtok.ap[:, tb * 8:(tb + 1) * 8], True, True)], [gtok, bconst], [pc])
            cp(gctok.ap, pc.ap[:, 0:32], [pc], [gctok])
            act(ekg.ap, pc.ap[:, 32:64], AF.Exp, [pc], [ekg])
            act(bgc.ap, pc.ap[:, 0:32], AF.Exp, [pc], [bgc])
            tt(bgc.ap, bgc.ap, btok.ap, ALU.mult, [bgc, btok], [bgc])

            def conv_silu(u, j, out):
                pp = ps[j % 2]
                proj16(pp, u)
                cp(pad.ap[:, 0:3], dn_tail_t[:, l, j, :], [bstate], [pad])
                act(pad.ap[:, 3:T + 3], pp.ap, AF.Copy, [pp], [pad])
                cp(dn_tail_t[:, l, j, :], pad.ap[:, T:T + 3], [pad], [bstate])
                ts(cv.ap, pad.ap[:, 3:T + 3], pvc(pslot, "dcw", j * 4 + 3), ALU.mult, [pad, pb_], [cv])
                for k in (2, 1, 0):
                    stt(cv.ap, pad.ap[:, k:k + T], pvc(pslot, "dcw", j * 4 + k), cv.ap, ALU.mult, ALU.add, [pad, cv, pb_], [cv])
                act(out.ap, cv.ap, AF.Silu, [cv], [out])

            def l2n(x_, scale):
                act(sqt.ap, x_.ap, AF.Square, [x_], [sqt])
                pn = ps[2]
                mms([(pn.ap, ones_f_t[:], sqt.ap, True, True)], [sqt, bconst], [pn])
                act(rn.ap, pn.ap, AF.Sqrt, [pn], [rn], bias=NORM_EPS)
                P.op("dve", lambda e: e.reciprocal(out=rn.ap, in_=rn.ap), bl([rn]), bl([rn]))
                stt(x_.ap, x_.ap, scale, rn.ap, ALU.mult, ALU.mult, [x_, rn], [x_])

            for h in range(8):
                bS = bdnS[l][h // 4]
                Sv = dnS_t[:, l, h, :]
                Sbv = dnSb_t[:, h, :]
                bSb = bdnSb[h // 4]
                act(Sbv, Sv, AF.Copy, [bS], [bSb])
                uq = wget(l, ("fm", C_DQKV + h * 128))
                conv_silu(uq, h, qn)
                l2n(qn, 128.0 ** -0.5)
                uk_ = wget(l, ("fm", C_DQKV + 1024 + h * 128))
                conv_silu(uk_, 8 + h, kn)
                l2n(kn, 1.0)
                cp(qnb.ap, qn.ap, [qn], [qnb]); cp(knb.ap, kn.ap, [kn], [knb], eng="act")
                pe_ = ps[3]; pbb = ps[4]
                for tb in range(4):
                    ts(diag.ap, ident_t[:], gctok.ap[:, tb * 8 + h: tb * 8 + h + 1], ALU.mult, [gctok, bconst], [diag])
                    mms([(pe_.ap[:, tb * 128:(tb + 1) * 128], ones_f_t[:], diag.ap, True, True)], [diag, bconst], [pe_])
                    ts(GM.ap, ident_t[:], bgc.ap[:, tb * 8 + h: tb * 8 + h + 1], ALU.mult, [bgc, bconst], [GM])
                    mms([(pbb.ap[:, tb * 128:(tb + 1) * 128], ones_f_t[:], GM.ap, True, True)], [GM, bconst], [pbb])
                act(Ebc.ap, pe_.ap, AF.Exp, [pe_], [Ebc])
                cp(egl.ap, Ebc.ap[:, 63:512:64], [Ebc], [egl])
                tt(qgT.ap, qn.ap, Ebc.ap, ALU.mult, [qn, Ebc], [qgT])
                tt(KbgT.ap, kn.ap, pbb.ap, ALU.mult, [kn, pbb], [KbgT])
                ptk = ps[5]
                for tb in range(4):
                    transp(ptk.ap[:, tb * 128:(tb + 1) * 128], kn.ap[:, tb * 128:(tb + 1) * 128], [kn], [ptk])
                for tb in range(4):
                    ts(kg[tb].ap, ptk.ap[:, tb * 128:(tb + 1) * 128], ekg.ap[:, tb * 8 + h: tb * 8 + h + 1], ALU.mult, [ptk, ekg], [kg[tb]])
                uv = wget(l, ("fm", C_DQKV + 2048 + h * 128))
                conv_silu(uv, 16 + h, sqt)
                for tb in range(4):
                    transp(ptk.ap[:, tb * 128:(tb + 1) * 128], sqt.ap[:, tb * 128:(tb + 1) * 128], [sqt], [ptk])
                for tb in range(4):
                    ts(Vb[tb].ap, ptk.ap[:, tb * 128:(tb + 1) * 128], btok.ap[:, tb * 8 + h: tb * 8 + h + 1], ALU.mult, [ptk, btok], [Vb[tb]])
                for tb in range(4):
                    tsl = slice(tb * 128, (tb + 1) * 128)
                    ts(GM.ap, UI_t[:], gtok.ap[:, tb * 8 + h: tb * 8 + h + 1], ALU.mult, [gtok, bconst], [GM])
                    pd = ps[2]
                    mms([(pd.ap[:, 0:128], GM.ap, ones_f_t[:], True, False), (pd.ap[:, 0:128], nones_f_t[:], GM.ap, False, True)], [GM, bconst], [pd])
                    ts(EA.ap, pd.ap[:, 0:128], 0.0, ALU.min, [pd], [EA])
                    act(EA.ap, EA.ap, AF.Exp, [EA], [EA])
                    ts(EB.ap, pd.ap[:, 0:128], -1.0, ALU.mult, [pd], [EB], s2=0.0, op1=ALU.min)
                    act(EB.ap, EB.ap, AF.Exp, [EB], [EB])
                    pg = ps[3]
                    mms([(pg.ap[:, 0:128], knb.ap[:, tsl], knb.ap[:, tsl], True, True)], [knb], [pg])
                    tt(EA.ap, EA.ap, LS_t[:], ALU.mult, [EA, bconst], [EA])
                    stt(NA.ap, pg.ap[:, 0:128], btok.ap[:, tb * 8 + h: tb * 8 + h + 1], EA.ap, ALU.mult, ALU.mult, [pg, btok, EA], [NA])
                    pq = ps[4]
                    mms([(pq.ap[:, 0:128], knb.ap[:, tsl], qnb.ap[:, tsl], True, True)], [knb, qnb], [pq])
                    tt(EB.ap, EB.ap, UI_t[:], ALU.mult, [EB, bconst], [EB])
                    tt(AqkT[tb].ap, pq.ap[:, 0:128], EB.ap, ALU.mult, [pq, EB], [AqkT[tb]])
                    pn_ = ps[5]
                    transp(pn_.ap[:, 0:128], NA.ap, [NA], [pn_])
                    cp(NB.ap, pn_.ap[:, 0:128], [pn_], [NB], eng="act")
                    tt(Rm.ap, ident_t[:], NB.ap, ALU.subtract, [NB, bconst], [Rm])
                    Ac, Bc, An, Bn = NA, NB, A2, B2
                    for lvl in range(5):
                        pA = ps[2]; pB = ps[3]; pR = ps[4]
                        mms([(pA.ap[:, 0:128], Bc.ap, Ac.ap, True, True)], [Ac, Bc], [pA])
                        if lvl < 4:
                            mms([(pB.ap[:, 0:128], Ac.ap, Bc.ap, True, True)], [Ac, Bc], [pB])
                        cp(An.ap, pA.ap[:, 0:128], [pA], [An])
                        if lvl < 4:
                            cp(Bn.ap, pB.ap[:, 0:128], [pB], [Bn], eng="act")
                        mms([(pR.ap[:, 0:128], An.ap, Rm.ap, True, True)], [An, Rm], [pR])
                        tt(Rm.ap, Rm.ap, pR.ap[:, 0:128], ALU.add, [Rm, pR], [Rm])
                        Ac, Bc, An, Bn = An, Bn, Ac, Bc
                    cp(TT[tb].ap, Rm.ap, [Rm], [TT[tb]], eng="act")
                po = ps[6]
                for n in range(8):
                    tb = n // 2; r0 = 64 * (n % 2)
                    csl = slice(n * 64, (n + 1) * 64); rs = slice(r0, r0 + 64)
                    p1 = ps[2]; p2 = ps[3]; p3 = ps[4]
                    mms([(p1.ap[rs, 0:128], KbgT.ap[:, csl], Sbv, True, True)], [KbgT, bSb], [p1])
                    tt(Wm.ap[rs, :], Vb[tb].ap[rs, :], p1.ap[rs, 0:128], ALU.subtract, [Vb[tb], p1], [Wm])
                    mms([(p2.ap[rs, 0:128], TT[tb].ap[rs, rs], Wm.ap[rs, :], True, True)], [TT[tb], Wm], [p2])
                    act(vnew.ap[rs, :], p2.ap[rs, 0:128], AF.Copy, [p2], [vnew])
                    mms([(po.ap[:, csl], Sbv, qgT.ap[:, csl], True, False), (po.ap[:, csl], vnew.ap[rs, :], AqkT[tb].ap[rs, rs], False, True)], [bSb, qgT, vnew, AqkT[tb]], [po])
                    mms([(p3.ap[:, 0:128], kg[tb].ap[rs, :], vnew.ap[rs, :], True, True)], [kg[tb], vnew], [p3])
                    stt(Sv, Sv, egl.ap[:, n:n + 1], p3.ap[:, 0:128], ALU.mult, ALU.add, [bS, egl, p3], [bS])
                    act(Sbv, Sv, AF.Copy, [bS], [bSb])
                act(cv.ap, po.ap, AF.Square, [po], [cv])
                pss = ps[7]
                mms([(pss.ap, ones_f_t[:], cv.ap, True, True)], [cv, bconst], [pss])
                act(rstd.ap, pss.ap, AF.Sqrt, [pss], [rstd], bias=NORM_EPS, scale=1.0 / 128.0)
                P.op("dve", lambda e: e.reciprocal(out=rstd.ap, in_=rstd.ap), bl([rstd]), bl([rstd]))
                uz = wget(l, ("fm", C_DZ + h * 128))
                pz = ps[h % 2]
                proj16(pz, uz)
                act(sz.ap, pz.ap, AF.Silu, [pz], [sz])
                stt(t1.ap, po.ap, pvc(pslot, "dng", 0), rstd.ap, ALU.mult, ALU.mult, [po, pb_, rstd], [t1])
                tt(ydn[h].ap, t1.ap, sz.ap, ALU.mult, [t1, sz], [ydn[h]])


        def phase_merge(l, pslot, ylru, ygla, ydn):
            A = Arena(TMP0)
            pb_ = pvb[pslot]
            merged = [A.bf16(T, "mg") for _ in range(16)]
            gsets = [[A.f32(T, "gt") for _ in range(3)] for _ in range(2)]
            acc = [A.f32(T, "acc") for _ in range(2)]; tmp = [A.f32(T, "tmp") for _ in range(2)]
            ys = (ylru, ygla, ydn)
            for m in range(16):
                gt = gsets[m % 2]
                for b in range(3):
                    ug = wget(l, ("fm", C_GATE + b * 2048 + m * 128))
                    pg = ps[b % 2]
                    proj16(pg, ug)
                    act(gt[b].ap, pg.ap, AF.Sigmoid, [pg, pb_], [gt[b]], bias=pvc(pslot, "bg", b * 16 + m))
                uA = wget(l, ("brA", m)); uB = wget(l, ("brB", m))
                ac = acc[m % 2]; tm = tmp[m % 2]
                for b in range(3):
                    u_, off = (uA, b * 1024) if b < 2 else (uB, 0)
                    pbr = ps[2 + b]
                    mms([(pbr.ap, u_.ap[:, off + cc * 128: off + (cc + 1) * 128], ys[b][cc].ap, cc == 0, cc == 7) for cc in range(8)], [u_] + ys[b], [pbr])
                    if b == 0:
                        tt(ac.ap, gt[0].ap, pbr.ap, ALU.mult, [gt[0], pbr], [ac])
                    else:
                        tt(tm.ap, gt[b].ap, pbr.ap, ALU.mult, [gt[b], pbr], [tm])
                        if b == 1:
                            tt(ac.ap, ac.ap, tm.ap, ALU.add, [ac, tm], [ac])
                        else:
                            tt(merged[m].ap, ac.ap, tm.ap, ALU.add, [ac, tm], [merged[m]])
            for m in range(16):
                uo = wget(l, ("out", m))
                po = ps[m % 2]
                mms([(po.ap, uo.ap[:, kc * 128:(kc + 1) * 128], merged[kc].ap, kc == 0, kc == KC - 1) for kc in range(KC)], [uo] + merged, [po])
                stt(xr[m].ap, xr[m].ap, ALPHA, po.ap, ALU.mult, ALU.add, [xr[m], po], [xr[m]])
            layernorm(A, pslot, "ln1g", "ln1b")

        def phase_mlp(l, pslot):
            A = Arena(0)
            pb_ = pvb[pslot]
            hh = [A.bf16(T, "hid") for _ in range(64)]
            rl = [A.f32(T, "rl") for _ in range(2)]
            for f_ in range(64):
                u1 = wget(l, ("w1", f_))
                p1 = ps[f_ % 2]
                proj16(p1, u1)
                r_ = rl[f_ % 2]
                act(r_.ap, p1.ap, AF.Relu, [p1, pb_], [r_], bias=pvc(pslot, "b1", f_))
                tt(hh[f_].ap, r_.ap, r_.ap, ALU.mult, [r_], [hh[f_]])
            for m in range(16):
                us = [wget(l, ("w2", m, j)) for j in range(4)]
                p2 = ps[2 + m % 2]
                mms([(p2.ap, us[f_ // 16].ap[:, (f_ % 16) * 128:(f_ % 16 + 1) * 128], hh[f_].ap, f_ == 0, f_ == 63) for f_ in range(64)], us + hh, [p2])
                r_ = rl[m % 2]
                act(r_.ap, p2.ap, AF.Identity, [p2, pb_], [r_], bias=pvc(pslot, "b2", m))
                stt(xr[m].ap, xr[m].ap, ALPHA, r_.ap, ALU.mult, ALU.add, [xr[m], r_], [xr[m]])
            layernorm(A, pslot, "ln2g", "ln2b", t1s=rl)

        PHASES = {}

        def main_loop(stop_after=None):
            npv = 0
            P.dma_in("sp", lambda e: e.dma_start(out=pv_t[:, 0, :], in_=pv_d[0]), pvb[0])
            for ti in range(ntiles):
                P.epoch = (ti * P.NEP) // ntiles
                for k in range(KC):
                    P.dma_in("sp", lambda e, ti=ti, k=k: e.dma_start(out=xr[k].ap, in_=xT_d[ti, :, k, :]), xr[k].b)
                for k in range(KC):
                    cp(xb[k].ap, xr[k].ap, [xr[k]], [xb[k]], eng=("act" if k % 2 else "dve"))
                for l in range(L):
                    pslot = npv % 2
                    npv += 1
                    if not (ti == ntiles - 1 and l == L - 1):
                        nl = (l + 1) % L
                        P.dma_in("sp", lambda e, nl=nl, s_=npv % 2: e.dma_start(out=pv_t[:, s_, :], in_=pv_d[nl]), pvb[npv % 2])
                    ylru, ygla, ydn = ybr(Y_LRU), ybr(Y_GLA), ybr(Y_DN)
                    PHASES["lru"](l, pslot, ylru)
                    P.barrier()
                    if stop_after == "lru":
                        wstate["next"] += NU - 17
                        wstate["issued"] = max(wstate["issued"], wstate["next"])
                    elif stop_after == "gla":
                        PHASES["gla"](l, pslot, ygla, ydn)
                        P.barrier()
                        wstate["next"] += NU - 17 - 29
                        wstate["issued"] = max(wstate["issued"], wstate["next"])
                    elif stop_after == "gdn":
                        PHASES["gla"](l, pslot, ygla)
                        P.barrier()
                        PHASES["gdn"](l, pslot, ydn)
                        P.barrier()
                        wstate["next"] += NU - 17 - 29 - 32
                        wstate["issued"] = max(wstate["issued"], wstate["next"])
                    else:
                        PHASES["gla"](l, pslot, ygla)
                        P.barrier()
                        PHASES["gdn"](l, pslot, ydn)
                        P.barrier()
                        PHASES["merge"](l, pslot, ylru, ygla, ydn)
                        P.barrier()
                        PHASES["mlp"](l, pslot)
                        P.barrier()
                    if dbg and ti == ntiles - 1 and l == 0:
                        for wi, yb_ in enumerate((ylru, ygla, ydn)):
                            for c in range(8):
                                P.dma_out("sp", lambda e, wi=wi, c=c, yb_=yb_: e.dma_start(out=dbg_d[wi, :, c, :], in_=yb_[c].ap), yb_[c].b, osem)
                for k in range(KC):
                    P.dma_out("sp", lambda e, ti=ti, k=k: e.dma_start(out=oT_d[ti, :, k, :], in_=xr[k].ap), xr[k].b, osem)
            for i_, (v_, n_) in enumerate(st_views[:3]):
                P.dma_out("sp", lambda e, v_=v_, o_=st_off[i_], n_=n_: e.dma_start(out=sto_d[:, o_:o_ + n_], in_=v_), bstate, osem)
            for l_ in range(L):
                for h_ in range(4):
                    o_ = st_off[3] + (l_ * 4 + h_) * 256
                    P.dma_out("sp", lambda e, l_=l_, h_=h_, o_=o_: e.dma_start(out=sto_d[:, o_:o_ + 256], in_=glaS_t[:, l_, h_, :]), bglaS[l_][h_], osem)
                for g_ in range(2):
                    o_ = st_off[4] + (l_ * 8 + g_ * 4) * 128
                    P.dma_out("sp", lambda e, l_=l_, g_=g_, o_=o_: e.dma_start(out=sto_d[:, o_:o_ + 512], in_=dnS_t[:, l_, g_ * 4:(g_ + 1) * 4, :].rearrange("p h v -> p (h v)")), bdnS[l_][g_], osem)
            P.wait_all("sp", osem)

        PHASES["lru"] = phase_lru
        for _k in ("gla", "gdn", "merge", "mlp"):
            if ("phase_" + _k) in locals():
                PHASES[_k] = locals()["phase_" + _k]
        main_loop(stop_after)
        P.emit()
    return nc


def _to_fm(x, ntiles):
    return np.ascontiguousarray(x.reshape(ntiles, T, KC, 128).transpose(0, 3, 2, 1))


def _from_fm(o):
    nt = o.shape[0]
    return o.transpose(0, 3, 2, 1).reshape(nt * T, D)


def kernel(x, w_in, lru_conv_w, lru_conv_b, lru_wa, lru_ba, lru_wi, lru_bi, lru_lambda,
           gla_wa2, gla_ba2, gla_norm_g, dn_conv_w, dn_a_log, dn_dt_bias, dn_norm_g,
           w_branch, b_gate, w_out, ln1_g, ln1_b, mlp_w1, mlp_b1, mlp_w2, mlp_b2, ln2_g, ln2_b):
    f = lambda a: np.asarray(a, dtype=np.float32)
    x = f(x)
    B, S, _ = x.shape
    L = w_in.shape[0]
    ntiles = S // T
    Ws = np.stack([pack_layer_units(f(w_in[l]), f(lru_wa[l]), f(lru_wi[l]), f(w_branch[l]), f(w_out[l]), f(mlp_w1[l]), f(mlp_w2[l])) for l in range(L)])
    pvs = np.stack([pack_pv(f(lru_conv_w[l]), f(lru_conv_b[l]), f(lru_ba[l]), f(lru_bi[l]), f(lru_lambda[l]), f(gla_wa2[l]), f(gla_ba2[l]), f(gla_norm_g[l]),
                            f(dn_conv_w[l]), f(dn_a_log[l]), f(dn_dt_bias[l]), f(dn_norm_g[l]), f(b_gate[l]), f(ln1_g[l]), f(ln1_b[l]), f(mlp_b1[l]), f(mlp_b2[l]),
                            f(ln2_g[l]), f(ln2_b[l])) for l in range(L)])
    NT_L = 8
    nlaunch = ntiles // NT_L
    nc = build_program(NT_L, L)
    n = 8
    NS = L * (8 + 24 + 72 + 1024 + 1024)
    states = [np.zeros((128, NS), np.float32) for _ in range(n)]
    xfm = [_to_fm(x[c % B], ntiles) for c in range(n)]
    outs = [[] for _ in range(n)]
    for li in range(nlaunch):
        in_maps = [{"xT": np.ascontiguousarray(xfm[c][li * NT_L:(li + 1) * NT_L]), "W": Ws, "pv": pvs, "st_in": states[c]} for c in range(n)]
        res = run_bass_kernel_spmd(nc, in_maps, core_ids=list(range(n)))
        for c in range(n):
            outs[c].append(np.asarray(res.results[c]["oT"]))
            states[c] = np.ascontiguousarray(np.asarray(res.results[c]["st_out"]))
    out = np.stack([_from_fm(np.concatenate(outs[c], axis=0)) for c in range(B)])
    return out.astype(np.float32)
```
